# Optimizing a Trainium2 kernel written in Bass

```python
import math
import jax
import jax.numpy as jnp
from jax import lax
import numpy as np

D_MODEL = 2048
BATCH = 4
SEQ = 4096
DEPTH = 4

D_FF = 11 * D_MODEL // 4
CHUNK = 128
A_WIDTH = D_MODEL // 2
A_GROUPS = 8
A_GROUP_CH = A_WIDTH // A_GROUPS
B_HEAD_DIM = 64
B_WIDTH = D_MODEL // 2
B_HEADS = B_WIDTH // (2 * B_HEAD_DIM)
C_INNER = D_MODEL
C_HEAD_DIM = 64
C_HEADS = C_INNER // C_HEAD_DIM
C_GROUPS = 4
C_STATE = 128
C_CONV = 4
C_CONV_CH = C_INNER + 2 * C_GROUPS * C_STATE
N_BRANCH = 3
REL_BUCKETS = 32
REL_MAX_DIST = 128
REL_EXACT = REL_BUCKETS // 2
IN_SPLITS = (A_WIDTH, A_WIDTH, B_WIDTH, B_WIDTH, B_WIDTH, C_INNER, C_CONV_CH, C_HEADS, N_BRANCH * D_MODEL)
D_IN = sum(IN_SPLITS)
EPS = 1e-6

kernel_name = 'hybrid_sgu_diffattn_ssd_macaron'


def rmsnorm(x, g):
    xf = x.astype(jnp.float32)
    y = xf * lax.rsqrt(jnp.mean(xf * xf, axis=-1, keepdims=True) + EPS)
    return (y * g.astype(jnp.float32)).astype(x.dtype)


def layernorm(x, g, b):
    xf = x.astype(jnp.float32)
    mu = jnp.mean(xf, axis=-1, keepdims=True)
    var = jnp.mean(jnp.square(xf - mu), axis=-1, keepdims=True)
    y = (xf - mu) * lax.rsqrt(var + EPS)
    return (y * g.astype(jnp.float32) + b.astype(jnp.float32)).astype(x.dtype)


def swiglu_ffn(x, wi, wo):
    gate, up = jnp.split(x @ wi, 2, axis=-1)
    return (jax.nn.silu(gate) * up) @ wo


def chunked_sgu(a_u, a_v, ln_g, ln_b, w_s, b_s):
    u = jax.nn.gelu(a_u)
    v = layernorm(jax.nn.gelu(a_v), ln_g, ln_b)
    b_, s_, _ = v.shape
    vc = v.reshape(b_, s_ // CHUNK, CHUNK, A_GROUPS, A_GROUP_CH)
    w = w_s * jnp.tril(jnp.ones((CHUNK, CHUNK), w_s.dtype))
    mixed = jnp.einsum('bnsgc,gts->bntgc', vc, w) + b_s.T[:, :, None]
    return u * mixed.reshape(b_, s_, A_WIDTH)


def t5_bucket(dist):
    n = jnp.maximum(dist, 0)
    nf = jnp.maximum(n, 1).astype(jnp.float32)
    large = REL_EXACT + (jnp.log(nf / REL_EXACT) / math.log(REL_MAX_DIST / REL_EXACT)
                         * (REL_BUCKETS - REL_EXACT)).astype(jnp.int32)
    large = jnp.minimum(large, REL_BUCKETS - 1)
    return jnp.where(n < REL_EXACT, n, large)


def diff_attention(q, k, v, lam, rel_bias):
    b_, s_, h_, _, dh = q.shape
    nb = s_ // CHUNK
    scale = dh ** -0.5
    k1 = k[..., 0, :]
    k2 = k[..., 1, :]
    qb = q.reshape(b_, nb, CHUNK, h_, 2, dh).swapaxes(0, 1)
    kpos = jnp.arange(s_)
    table = rel_bias.astype(jnp.float32)

    def block(args):
        qblk, i = args
        qpos = i * CHUNK + jnp.arange(CHUNK)
        dist = qpos[:, None] - kpos[None, :]
        bias = table[t5_bucket(dist)].transpose(2, 0, 1)
        causal = dist >= 0

        def probs(qq, kk):
            s = jnp.einsum('bqhd,bkhd->bhqk', qq, kk).astype(jnp.float32) * scale + bias
            return jax.nn.softmax(jnp.where(causal, s, -jnp.inf), axis=-1)

        p = probs(qblk[..., 0, :], k1) - lam * probs(qblk[..., 1, :], k2)
        return jnp.einsum('bhqk,bkhe->bqhe', p.astype(v.dtype), v)

    out = lax.map(block, (qb, jnp.arange(nb)))
    return out.swapaxes(0, 1).reshape(b_, s_, h_, 2 * dh)


def ssd(x, dt, A, Bm, Cm):
    f32 = jnp.float32
    b_, s_, h_, p_ = x.shape
    g_, n_ = Bm.shape[2], Bm.shape[3]
    r_ = h_ // g_
    c_ = s_ // CHUNK
    xd = (x.astype(f32) * dt[..., None]).reshape(b_, c_, CHUNK, g_, r_, p_)
    a_cs = jnp.cumsum((dt * A).reshape(b_, c_, CHUNK, g_, r_), axis=2)
    Bc = Bm.astype(f32).reshape(b_, c_, CHUNK, g_, n_)
    Cc = Cm.astype(f32).reshape(b_, c_, CHUNK, g_, n_)
    mask = jnp.tril(jnp.ones((CHUNK, CHUNK), bool))[None, None, :, :, None, None]
    seg = a_cs[:, :, :, None] - a_cs[:, :, None, :]
    decay = jnp.exp(jnp.where(mask, seg, -jnp.inf))
    cb = jnp.einsum('bclgn,bcsgn->bclsg', Cc, Bc)
    y_diag = jnp.einsum('bclsgr,bcsgrp->bclgrp', cb[..., None] * decay, xd)
    decay_states = jnp.exp(a_cs[:, :, -1:] - a_cs)
    states = jnp.einsum('bcsgn,bcsgrp->bcgrpn', Bc, xd * decay_states[..., None])
    chunk_decay = jnp.exp(a_cs[:, :, -1])

    def step(h, inp):
        st, dec = inp
        return h * dec[..., None, None] + st, h

    _, prev = lax.scan(step, jnp.zeros_like(states[:, 0]),
                       (jnp.moveaxis(states, 1, 0), jnp.moveaxis(chunk_decay, 1, 0)))
    prev = jnp.moveaxis(prev, 0, 1)
    y_off = jnp.einsum('bclgn,bcgrpn->bclgrp', Cc, prev) * jnp.exp(a_cs)[..., None]
    return (y_diag + y_off).reshape(b_, s_, h_, p_)


def mamba2_mixer(z, xbc, dt_raw, conv_w, conv_b, dt_bias, a_log, d_skip, norm_g):
    xbc = lax.conv_general_dilated(xbc, conv_w.astype(xbc.dtype)[:, None, :], (1,), [(C_CONV - 1, 0)],
                                   dimension_numbers=('NWC', 'WIO', 'NWC'),
                                   feature_group_count=C_CONV_CH)
    xbc = jax.nn.silu(xbc + conv_b)
    xs, Bm, Cm = jnp.split(xbc, [C_INNER, C_INNER + C_GROUPS * C_STATE], axis=-1)
    b_, s_, _ = xs.shape
    xs = xs.reshape(b_, s_, C_HEADS, C_HEAD_DIM)
    Bm = Bm.reshape(b_, s_, C_GROUPS, C_STATE)
    Cm = Cm.reshape(b_, s_, C_GROUPS, C_STATE)
    dt = jax.nn.softplus(dt_raw.astype(jnp.float32) + dt_bias.astype(jnp.float32))
    A = -jnp.exp(a_log.astype(jnp.float32))
    y = ssd(xs, dt, A, Bm, Cm) + d_skip.astype(jnp.float32)[:, None] * xs.astype(jnp.float32)
    y = y.reshape(b_, s_, C_INNER).astype(z.dtype)
    return rmsnorm(y * jax.nn.silu(z), norm_g)


def hybrid_mixer(xn, rel_bias, w_in, sgu_ln_g, sgu_ln_b, sgu_w, sgu_b, diff_lambda, diff_subln,
                 conv_w, conv_b, dt_bias, a_log, d_skip, ssm_norm, w_pa, w_pb, w_pc, w_out, lam_init):
    b_, s_, _ = xn.shape
    split_idx = [int(i) for i in np.cumsum(IN_SPLITS)[:-1]]
    a_u, a_v, q, k, v, z, xbc, dt_raw, gate_logits = jnp.split(xn @ w_in, split_idx, axis=-1)
    y_a = chunked_sgu(a_u, a_v, sgu_ln_g, sgu_ln_b, sgu_w, sgu_b)
    lq1, lk1, lq2, lk2 = diff_lambda.astype(jnp.float32)
    lam = jnp.exp(jnp.sum(lq1 * lk1)) - jnp.exp(jnp.sum(lq2 * lk2)) + lam_init
    attn = diff_attention(q.reshape(b_, s_, B_HEADS, 2, B_HEAD_DIM),
                          k.reshape(b_, s_, B_HEADS, 2, B_HEAD_DIM),
                          v.reshape(b_, s_, B_HEADS, 2 * B_HEAD_DIM), lam, rel_bias)
    y_b = (rmsnorm(attn, diff_subln) * (1.0 - lam_init)).reshape(b_, s_, B_WIDTH)
    y_c = mamba2_mixer(z, xbc, dt_raw, conv_w, conv_b, dt_bias, a_log, d_skip, ssm_norm)
    gates = jax.nn.sigmoid(gate_logits.astype(jnp.float32)).astype(xn.dtype).reshape(b_, s_, N_BRANCH, D_MODEL)
    merged = (gates[..., 0, :] * (y_a @ w_pa) + gates[..., 1, :] * (y_b @ w_pb)
              + gates[..., 2, :] * (y_c @ w_pc))
    return merged @ w_out


def setup_inputs(seed: int = 0) -> dict:
    key = jax.random.key(seed)
    ks = iter(jax.random.split(key, 32))
    f32 = jnp.float32
    L = DEPTH

    def nrm(shape, scale):
        return jax.random.normal(next(ks), shape, f32) * scale

    def gain(shape):
        return 1.0 + 0.02 * jax.random.normal(next(ks), shape, f32)

    x = nrm((BATCH, SEQ, D_MODEL), 1.0)
    rel_bias = nrm((REL_BUCKETS, B_HEADS), 0.5)
    final_norm = gain((D_MODEL,))
    ffn1_norm = gain((L, D_MODEL))
    ffn1_wi = nrm((L, D_MODEL, 2 * D_FF), D_MODEL ** -0.5)
    ffn1_wo = nrm((L, D_FF, D_MODEL), D_FF ** -0.5)
    mix_norm = gain((L, D_MODEL))
    w_in = nrm((L, D_MODEL, D_IN), D_MODEL ** -0.5)
    sgu_ln_g = gain((L, A_WIDTH))
    sgu_ln_b = nrm((L, A_WIDTH), 0.02)
    sgu_w = nrm((L, A_GROUPS, CHUNK, CHUNK), CHUNK ** -0.5)
    sgu_b = gain((L, A_GROUPS, CHUNK))
    diff_lambda = nrm((L, 4, B_HEAD_DIM), 0.1)
    diff_subln = gain((L, 2 * B_HEAD_DIM))
    conv_w = nrm((L, C_CONV, C_CONV_CH), C_CONV ** -0.5)
    conv_b = nrm((L, C_CONV_CH), 0.02)
    u = jax.random.uniform(next(ks), (L, C_HEADS), f32)
    dt0 = jnp.exp(u * (math.log(0.1) - math.log(0.001)) + math.log(0.001))
    dt_bias = dt0 + jnp.log(-jnp.expm1(-dt0))
    a_log = jnp.log(jax.random.uniform(next(ks), (L, C_HEADS), f32, minval=1.0, maxval=16.0))
    d_skip = gain((L, C_HEADS))
    ssm_norm = gain((L, C_INNER))
    w_pa = nrm((L, A_WIDTH, D_MODEL), A_WIDTH ** -0.5)
    w_pb = nrm((L, B_WIDTH, D_MODEL), B_WIDTH ** -0.5)
    w_pc = nrm((L, C_INNER, D_MODEL), C_INNER ** -0.5)
    w_out = nrm((L, D_MODEL, D_MODEL), D_MODEL ** -0.5)
    ffn2_norm = gain((L, D_MODEL))
    ffn2_wi = nrm((L, D_MODEL, 2 * D_FF), D_MODEL ** -0.5)
    ffn2_wo = nrm((L, D_FF, D_MODEL), D_FF ** -0.5)
    return {'x': x, 'rel_bias': rel_bias, 'final_norm': final_norm,
            'ffn1_norm': ffn1_norm, 'ffn1_wi': ffn1_wi, 'ffn1_wo': ffn1_wo,
            'mix_norm': mix_norm, 'w_in': w_in,
            'sgu_ln_g': sgu_ln_g, 'sgu_ln_b': sgu_ln_b, 'sgu_w': sgu_w, 'sgu_b': sgu_b,
            'diff_lambda': diff_lambda, 'diff_subln': diff_subln,
            'conv_w': conv_w, 'conv_b': conv_b, 'dt_bias': dt_bias, 'a_log': a_log,
            'd_skip': d_skip, 'ssm_norm': ssm_norm,
            'w_pa': w_pa, 'w_pb': w_pb, 'w_pc': w_pc, 'w_out': w_out,
            'ffn2_norm': ffn2_norm, 'ffn2_wi': ffn2_wi, 'ffn2_wo': ffn2_wo}


def reference(x, rel_bias, final_norm, ffn1_norm, ffn1_wi, ffn1_wo, mix_norm, w_in,
              sgu_ln_g, sgu_ln_b, sgu_w, sgu_b, diff_lambda, diff_subln,
              conv_w, conv_b, dt_bias, a_log, d_skip, ssm_norm,
              w_pa, w_pb, w_pc, w_out, ffn2_norm, ffn2_wi, ffn2_wo):
    h = x
    for l in range(DEPTH):
        lam_init = 0.8 - 0.6 * math.exp(-0.3 * l)
        h = h + 0.5 * swiglu_ffn(rmsnorm(h, ffn1_norm[l]), ffn1_wi[l], ffn1_wo[l])
        h = h + hybrid_mixer(rmsnorm(h, mix_norm[l]), rel_bias, w_in[l],
                             sgu_ln_g[l], sgu_ln_b[l], sgu_w[l], sgu_b[l],
                             diff_lambda[l], diff_subln[l],
                             conv_w[l], conv_b[l], dt_bias[l], a_log[l], d_skip[l], ssm_norm[l],
                             w_pa[l], w_pb[l], w_pc[l], w_out[l], lam_init)
        h = h + 0.5 * swiglu_ffn(rmsnorm(h, ffn2_norm[l]), ffn2_wi[l], ffn2_wo[l])
    return rmsnorm(h, final_norm)
```

```python
import math
import os
import contextlib
import numpy as np
import concourse.bass as bass
import concourse.mybir as mybir
from concourse.bass_utils import run_bass_kernel_spmd

F32 = mybir.dt.float32
BF16 = mybir.dt.bfloat16
U8 = mybir.dt.uint8
AF = mybir.ActivationFunctionType
ALU = mybir.AluOpType
AX = mybir.AxisListType
EPS = 1e-6
NEG = -30000.0
PE_ALT = os.environ.get('KALT', 'dve')


def _esz(dt):
    return 4 if dt == F32 else (2 if dt == BF16 else 1)


class Cfg:
    def __init__(self, D=2048, S=4096, L=4, B=4, TG=2):
        self.D, self.S, self.L, self.B = D, S, L, B
        self.DFF = 11 * D // 4
        self.AW = D // 2
        self.AG = self.AW // 128
        self.BW = D // 2
        self.BH = self.BW // 128
        self.CI = D
        self.CH = self.CI // 64
        self.CG = 4
        self.R = self.CH // self.CG
        self.CC = self.CI + 2 * self.CG * 128
        self.DIN = 2 * self.AW + 3 * self.BW + self.CI + self.CC + self.CH + 3 * D
        self.KD = D // 128
        self.KF = self.DFF // 128
        self.NT = S
        self.TG = TG
        self.T = 512 * TG
        self.NCC = self.CC // 128
        self.o_au = 0
        self.o_av = self.AW
        self.o_q = 2 * self.AW
        self.o_k = self.o_q + self.BW
        self.o_v = self.o_k + self.BW
        self.o_z = self.o_v + self.BW
        self.o_xbc = self.o_z + self.CI
        self.o_dt = self.o_xbc + self.CC
        self.o_g = self.o_dt + self.CH
        L_, KD, NCC = L, self.KD, self.NCC
        c = 0
        self.cv_f1 = c; c += L_ * KD
        self.cv_mix = c; c += L_ * KD
        self.cv_f2 = c; c += L_ * KD
        self.cv_fin = c; c += KD
        self.cv_cw = c; c += L_ * 4 * NCC
        self.cv_cb = c; c += L_ * NCC
        self.cv_sub = c; c += L_
        self.cv_c31 = c; c += self.BH
        self.cv_lamc = c; c += 3 * L_
        self.NCV = c
        r = 0
        self.rv_lng = r; r += self.AW
        self.rv_lnb = r; r += self.AW
        self.rv_dtb = r; r += self.CH
        self.rv_alog = r; r += self.CH
        self.rv_dsk = r; r += self.CH
        self.rv_ssm = r; r += self.CI
        self.rv_lam = r; r += 256
        self.NRV = r


ENGS = ('pe', 'act', 'dve', 'pool', 'sp')


class Prog:
    def __init__(self, nc, big, ps, sem_handles, sbuf_bytes):
        self.nc = nc
        self.big = big
        self.ps = ps
        self.ops = {e: [] for e in ENGS}
        self.cnt = {e: 0 for e in ENGS}
        self.seen = {e: {} for e in ENGS}
        self.sem = {}
        it = iter(sem_handles)
        for e in ('pe', 'act', 'dve', 'pool'):
            self.sem[('eng', e)] = next(it)
        self.dq = {}
        for q, K in (('sp', int(os.environ.get('KSP', '4'))), ('pool', 12), ('act', 4)):
            self.dq[q] = {'n': 0, 'K': K}
            for i in range(K):
                self.sem[('dma', q, i)] = next(it)
        self.recs = {}
        self.dcols = {}
        self.ro = set()
        self.sp_ = 0
        self.sbuf_bytes = sbuf_bytes
        self.last_dma = {}
        self.pos = {}
        self.gi = 0
        self.dump = [] if os.environ.get('KDUMP') else None
        self.limit = int(os.environ.get('KLIMIT', '0')) or 10**12

    def alloc(self, free_shape, dt):
        n = 1
        for s in free_shape:
            n *= s
        nb = n * _esz(dt)
        nb = (nb + 63) // 64 * 64
        off = self.sp_
        self.sp_ += nb
        assert self.sp_ <= self.sbuf_bytes, ("SBUF overflow", self.sp_)
        v = self.big[:, off:off + n * _esz(dt)].bitcast(dt)
        if len(free_shape) == 2:
            v = v.rearrange('p (a b) -> p a b', b=free_shape[1])
        elif len(free_shape) == 3:
            v = v.rearrange('p (a b c) -> p a b c', b=free_shape[1], c=free_shape[2])
        return v

    def mark(self):
        return self.sp_

    def release(self, m):
        self.sp_ = m

    def bank(self, i, dt=F32):
        v = self.ps[:, i, :]
        if dt != F32:
            v = v.bitcast(dt)
        return v

    def dram(self, name, rows, cols, dt, kind='Internal', ro=False):
        t = self.nc.dram_tensor(name, [rows, cols], dt, kind=kind).ap()
        self.dcols[name] = cols
        if ro:
            self.ro.add(name)
        return t

    def _reg(self, ap):
        name = ap.tensor.name
        es = _esz(ap.dtype)
        pat = ap.ap
        off = ap.offset
        if name in self.dcols:
            ncols = self.dcols[name]
            r0 = off // ncols
            c0 = off % ncols
            rext = 0
            cext = 0
            for st, cn in pat:
                if cn <= 1 or st == 0:
                    continue
                if st % ncols == 0:
                    rext += (cn - 1) * (st // ncols)
                else:
                    cext += (cn - 1) * st
            assert c0 + cext < ncols, (name, off, pat)
            page = max(512, (ncols * es) // 16)
            return (name, r0, r0 + rext + 1, c0 * es, (c0 + cext + 1) * es, page)
        pstep = pat[0][0]
        r0 = off // pstep
        f0 = off % pstep
        ext = 0
        for st, cn in pat[1:]:
            if cn > 1 and st > 0:
                ext += (cn - 1) * st
        return (name, r0, r0 + pat[0][1], f0 * es, (f0 + ext + 1) * es, 2048)

    def _deps(self, eng, reads, writes, semkey, val):
        deps = {}
        rregs = [self._reg(a) for a in reads if a.tensor.name not in self.ro]
        wregs = [self._reg(a) for a in writes]
        for regs, isw in ((rregs, False), (wregs, True)):
            for (name, r0, r1, c0, c1, page) in regs:
                pages = self.recs.setdefault(name, {})
                for pg in range(c0 // page, (c1 - 1) // page + 1):
                    lst = pages.get(pg)
                    if not lst:
                        continue
                    for rec in lst:
                        if rec[0] >= r1 or rec[1] <= r0 or rec[2] >= c1 or rec[3] <= c0:
                            continue
                        if (not isw) and (not rec[4]):
                            continue
                        if rec[7] == 'pe' and eng == 'pe':
                            continue
                        if rec[7] == eng and eng != 'dmaq' and isw and not rec[4]:
                            continue
                        k = rec[5]
                        if deps.get(k, 0) < rec[6]:
                            deps[k] = rec[6]
        for regs, isw in ((rregs, False), (wregs, True)):
            for (name, r0, r1, c0, c1, page) in regs:
                pages = self.recs.setdefault(name, {})
                newrec = (r0, r1, c0, c1, isw, semkey, val, eng)
                for pg in range(c0 // page, (c1 - 1) // page + 1):
                    lst = pages.get(pg)
                    if lst is None:
                        pages[pg] = [newrec]
                        continue
                    if isw:
                        lst[:] = [r for r in lst if not (r[0] >= r0 and r[1] <= r1 and r[2] >= c0 and r[3] <= c1)]
                    else:
                        lst[:] = [r for r in lst if not ((not r[4]) and r[5] == semkey and r[0] >= r0 and r[1] <= r1 and r[2] >= c0 and r[3] <= c1)]
                    lst.append(newrec)
        return deps

    def _waits(self, eng, deps):
        out = []
        seen = self.seen[eng]
        for k, v in deps.items():
            if seen.get(k, 0) >= v:
                continue
            seen[k] = v
            out.append((k, v))
            if k[0] == 'eng':
                self.ops[k[1]][self.pos[k[1]][v]][3] = True
        return out

    def op(self, eng, fn, reads, writes):
        self.cnt[eng] += 1
        semkey = ('eng', eng)
        val = self.cnt[eng]
        deps = self._deps(eng, reads, writes, semkey, val)
        waits = self._waits(eng, deps)
        self.pos.setdefault(eng, {})[val] = len(self.ops[eng])
        self.gi += 1
        if self.dump is not None:
            import sys as _s
            self.dump.append((self.gi, eng, _s._getframe(2).f_lineno, list(waits), val))
        self.ops[eng].append([waits, fn, (semkey, 1), False, self.gi])

    def dma(self, q, out, in_):
        d = self.dq[q]
        i = d['n'] % d['K']
        rnd = d['n'] // d['K']
        d['n'] += 1
        semkey = ('dma', q, i)
        val = 16 * (rnd + 1)
        deps = self._deps('dmaq', [in_], [out], semkey, val)
        if rnd > 0:
            deps[semkey] = max(deps.get(semkey, 0), 16 * rnd)
        waits = self._waits(q, deps)
        self.gi += 1
        if self.dump is not None:
            import sys as _s
            self.dump.append((self.gi, 'dma-' + q, _s._getframe(1).f_lineno, list(waits), (semkey, val)))
        self.ops[q].append([waits, lambda e: e.dma_start(out=out, in_=in_), (semkey, 16, val), True, self.gi])
        self.last_dma[semkey] = val

    def finish(self):
        waits = self._waits('sp', dict(self.last_dma))
        self.ops['sp'].append([waits, None, None, False, 0])

    def emit(self, block):
        P = self

        vmap = {}
        for en in ('pe', 'act', 'dve', 'pool'):
            m = {}
            run_ = 0
            inv = {p: v for v, p in P.pos.get(en, {}).items()}
            for p, o in enumerate(P.ops[en]):
                if o[2] is not None and o[2][0][0] == 'eng' and o[3] and o[4] <= P.limit:
                    run_ += 1
                    m[inv[p]] = run_
            vmap[en] = m
        P.n_inc = {en: len(vmap[en]) for en in vmap}
        P.emitted = {}

        def run(e, lst):
            for waits, fn, inc, ms, gi in lst:
                if gi > P.limit:
                    continue
                if fn is None:
                    waits = list(P.emitted.items())
                elif inc[0][0] == 'dma':
                    P.emitted[inc[0]] = max(P.emitted.get(inc[0], 0), inc[2])
                for k, v in waits:
                    if k[0] == 'eng':
                        v = vmap[k[1]][v]
                    e.wait_ge(P.sem[k], v)
                if fn is None:
                    continue
                ins = fn(e)
                if ms:
                    ins.then_inc(P.sem[inc[0]], inc[1])

        @block.tensor
        def _(e):
            run(e, P.ops['pe'])

        @block.scalar
        def _(e):
            run(e, P.ops['act'])

        @block.vector
        def _(e):
            run(e, P.ops['dve'])

        @block.gpsimd
        def _(e):
            run(e, P.ops['pool'])

        @block.sync
        def _(e):
            run(e, P.ops['sp'])

    def mm(self, out, lhsT, rhs, start=True, stop=True):
        self.op('pe', lambda e: e.matmul(out, lhsT=lhsT, rhs=rhs, start=start, stop=stop), [lhsT, rhs], [out])

    def tr(self, out, in_, ident):
        self.op('pe', lambda e: e.transpose(out, in_, ident), [in_, ident], [out])

    def act(self, out, in_, func, bias=None, scale=None):
        kw = {}
        rd = [in_]
        if bias is not None:
            kw['bias'] = bias
            if not isinstance(bias, (int, float)):
                rd.append(bias)
        if scale is not None:
            kw['scale'] = scale
            if not isinstance(scale, (int, float)):
                rd.append(scale)
        self.op('act', lambda e: e.activation(out=out, in_=in_, func=func, **kw), rd, [out])

    def tt(self, eng, out, in0, in1, op):
        self.op(eng, lambda e: e.tensor_tensor(out=out, in0=in0, in1=in1, op=op), [in0, in1], [out])

    def ts(self, eng, out, in0, s1, s2, op0, op1):
        rd = [in0] + [s for s in (s1, s2) if not isinstance(s, (int, float))]
        self.op(eng, lambda e: e.tensor_scalar(out=out, in0=in0, scalar1=s1, scalar2=s2, op0=op0, op1=op1), rd, [out])

    def ts1(self, eng, out, in0, s1, op0):
        rd = [in0] + ([] if isinstance(s1, (int, float)) else [s1])
        self.op(eng, lambda e: e.tensor_single_scalar(out=out, in_=in0, scalar=s1, op=op0), rd, [out])

    def stt(self, eng, out, in0, scalar, in1, op0, op1):
        rd = [in0, in1] + ([] if isinstance(scalar, (int, float)) else [scalar])
        self.op(eng, lambda e: e.scalar_tensor_tensor(out=out, in0=in0, scalar=scalar, in1=in1, op0=op0, op1=op1), rd, [out])

    def copy(self, eng, out, in_):
        if eng == 'act':
            self.op('act', lambda e: e.copy(out=out, in_=in_), [in_], [out])
        else:
            self.op(eng, lambda e: e.tensor_copy(out=out, in_=in_), [in_], [out])

    def recip(self, out, in_):
        self.op('dve', lambda e: e.reciprocal(out=out, in_=in_), [in_], [out])

    def rsum(self, out, in_):
        self.op('dve', lambda e: e.reduce_sum(out=out, in_=in_, axis=AX.X), [in_], [out])

    def memset(self, eng, ap, val):
        self.op(eng, lambda e: e.memset(ap, val), [], [ap])


class WStream:
    NB = 3

    def __init__(self, P, bufs, loaders):
        self.P, self.bufs, self.loaders = P, bufs, loaders
        self.next = 0

    def get(self, i, live=1):
        lim = min(i + self.NB - live, len(self.loaders) - 1)
        while self.next <= lim:
            self.loaders[self.next](self.bufs[self.next % self.NB])
            self.next += 1
        return self.bufs[i % self.NB]


def bc(ap, shape):
    return ap.broadcast_to(list(shape))


def build(cfg, n_layers=None, stop_after=None, emit_h=False):
    c = cfg
    D, L, NT, T, TG, KD, KF = c.D, c.L, c.NT, c.T, c.TG, c.KD, c.KF
    if n_layers is None:
        n_layers = L
    nc = bass.Bass('TRN2', target_bir_lowering=False)
    SB = 204 * 1024

    class G:
        pass
    g = G()

    with contextlib.ExitStack() as es:
        big = es.enter_context(nc.sbuf_tensor('big', [128, SB], U8))
        ps = es.enter_context(nc.psum_tensor('ps', [128, 8, 512], F32))
        sems = [es.enter_context(nc.semaphore('s%d' % i)) for i in range(4 + 16 + 12 + 4)]
        block = es.enter_context(nc.Block())
        P = Prog(nc, big, ps, sems, SB)

        def din(name, rows, cols, dt=F32):
            return P.dram(name, rows, cols, dt, kind='ExternalInput', ro=True)
        xT = din('xT', D, NT)
        f1wi = din('f1wi', L * D, 2 * c.DFF)
        f1wo = din('f1wo', L * c.DFF, D)
        f2wi = din('f2wi', L * D, 2 * c.DFF)
        f2wo = din('f2wo', L * c.DFF, D)
        win = din('win', L * D, c.DIN)
        wpa = din('wpa', L * c.AW, D)
        wpb = din('wpb', L * c.BW, D)
        wpc = din('wpc', L * c.CI, D)
        wout = din('wout', L * D, D)
        colvec = din('colvec', 128, c.NCV)
        rowvec = din('rowvec', 128, L * c.NRV)
        sguwT = din('sguwT', L * c.AG * 128, 128)
        sgub = din('sgub', L, c.AG * 128)
        biasnear = din('biasnear', c.BH * 128, 5 * 512)
        cst = din('cst', 128, 3 * 128)
        outT = P.dram('outT', D, NT, F32, kind='ExternalOutput')
        hT = P.dram('hT', D, NT, F32, kind=('ExternalOutput' if emit_h else 'Internal'))
        uT = P.dram('uT', c.AW, NT, F32)
        vS = P.dram('vS', NT, c.AW, BF16)
        qT = P.dram('qT', c.BW, NT, BF16)
        kT = P.dram('kT', c.BW, NT, BF16)
        vA = P.dram('vA', NT, c.BW, BF16)
        szD = P.dram('szD', NT, c.CI, F32)
        xbcT = P.dram('xbcT', c.CC, NT + 4, F32)
        dtr = P.dram('dtr', NT, c.CH, F32)
        gT = P.dram('gT', 3 * D, NT, F32)
        yaT = P.dram('yaT', c.AW, NT, BF16)
        ybT = P.dram('ybT', c.BW, NT, BF16)
        ycT = P.dram('ycT', c.CI, NT, BF16)
        f1wi_b = P.dram('f1wi_b', D, 2 * c.DFF, BF16)
        f1wo_b = P.dram('f1wo_b', c.DFF, D, BF16)
        f2wi_b = P.dram('f2wi_b', D, 2 * c.DFF, BF16)
        f2wo_b = P.dram('f2wo_b', c.DFF, D, BF16)
        win_b = P.dram('win_b', D, c.DIN, BF16)
        wpa_b = P.dram('wpa_b', c.AW, D, BF16)
        wpb_b = P.dram('wpb_b', c.BW, D, BF16)
        wpc_b = P.dram('wpc_b', c.CI, D, BF16)
        wout_b = P.dram('wout_b', D, D, BF16)
        castmap = [(f1wi, f1wi_b, D, 2 * c.DFF), (f1wo, f1wo_b, c.DFF, D), (win, win_b, D, c.DIN),
                   (wpa, wpa_b, c.AW, D), (wpb, wpb_b, c.BW, D), (wpc, wpc_b, c.CI, D), (wout, wout_b, D, D),
                   (f2wi, f2wi_b, D, 2 * c.DFF), (f2wo, f2wo_b, c.DFF, D)]

        def cast_phase(l):
            m0 = P.mark()
            CW = 4096
            s32 = [P.alloc([CW], F32) for _ in range(2)]
            s16 = [P.alloc([CW], BF16) for _ in range(2)]
            it = 0
            for (src, dst, nr, ncl) in castmap:
                for r in range(0, nr, 128):
                    for c0 in range(0, ncl, CW):
                        w = min(CW, ncl - c0)
                        a32, a16 = s32[it % 2], s16[it % 2]
                        it += 1
                        P.dma('sp', a32[:, 0:w], src[l * nr + r:l * nr + r + 128, c0:c0 + w])
                        P.copy('act', a16[:, 0:w], a32[:, 0:w])
                        P.dma('sp', dst[r:r + 128, c0:c0 + w], a16[:, 0:w])
            P.release(m0)

        cv = P.alloc([c.NCV], F32)
        csts = P.alloc([3, 128], F32)
        ones_f = P.alloc([128], F32)
        ones_b = P.alloc([128], BF16)
        ident_b = P.alloc([128], BF16)
        wbufs = [P.alloc([8192], BF16) for _ in range(WStream.NB)]
        P.dma('sp', cv, colvec)
        P.dma('sp', csts, cst.rearrange('p (a b) -> p a b', b=128))
        tri = csts[:, 0, :]
        maskneg = csts[:, 1, :]
        ident_f = csts[:, 2, :]
        P.memset('dve', ones_f, 1.0)
        P.memset('dve', ones_b, 1.0)
        P.copy('dve', ident_b, ident_f)
        zt = P.alloc([4], F32)
        P.memset('dve', zt, 0.0)
        for ch in range(c.NCC):
            P.dma('sp', xbcT[ch * 128:(ch + 1) * 128, 0:4], zt)

        def wload(buf_view, src2d, r0, nrows, c0, ncols):
            P.dma('sp', buf_view, src2d[r0:r0 + nrows, c0:c0 + ncols].rearrange('(k p) n -> p k n', p=128))

        def wview(buf, nk, ncols):
            return buf[:, 0:nk * ncols].rearrange('p (k n) -> p k n', n=ncols)

        def norm_to_xn(src, t0, gcol, xn, hb, rstd, Tn, out_f32=None):
            tgn = Tn // 512
            for k in range(KD):
                hk = hb[k % 2]
                P.dma('sp', hk[:, 0:Tn], src[k * 128:(k + 1) * 128, t0:t0 + Tn])
                P.act(hk[:, 0:Tn], hk[:, 0:Tn], AF.Square)
                for tg in range(tgn):
                    P.mm(P.bank(6 + tg), ones_f, hk[:, tg * 512:(tg + 1) * 512], start=(k == 0), stop=(k == KD - 1))
            for tg in range(tgn):
                P.act(rstd[:, tg * 512:(tg + 1) * 512], P.bank(6 + tg), AF.Sqrt, scale=1.0 / D, bias=EPS)
            P.recip(rstd[:, 0:Tn], rstd[:, 0:Tn])
            for k in range(KD):
                hk = hb[k % 2]
                P.dma('sp', hk[:, 0:Tn], src[k * 128:(k + 1) * 128, t0:t0 + Tn])
                if out_f32 is None:
                    P.stt('dve', xn[:, k, 0:Tn], hk[:, 0:Tn], cv[:, gcol + k:gcol + k + 1], rstd[:, 0:Tn], ALU.mult, ALU.mult)
                else:
                    P.stt('dve', hk[:, 0:Tn], hk[:, 0:Tn], cv[:, gcol + k:gcol + k + 1], rstd[:, 0:Tn], ALU.mult, ALU.mult)
                    P.dma('sp', out_f32[k * 128:(k + 1) * 128, t0:t0 + Tn], hk[:, 0:Tn])

        def ffn_phase(l, wi2d, wo2d, gcol0, src):
            m0 = P.mark()
            xn = P.alloc([KD, T], BF16)
            actT = P.alloc([KF, T], BF16)
            hb = [P.alloc([T], F32) for _ in range(2)]
            rstd = P.alloc([T], F32)
            stmp = [P.alloc([512], F32) for _ in range(2)]
            hres = [P.alloc([512], F32) for _ in range(2)]
            hnew = [P.alloc([512], F32) for _ in range(2)]
            CB = 2 if KF % 2 == 0 else 1
            nblk = KF // CB
            npc = (KF + 21) // 22
            pcs = []
            kb = 0
            for pc in range(npc):
                nk = min(22, KF - kb)
                pcs.append((kb, nk))
                kb += nk
            for t0 in range(0, NT, T):
                norm_to_xn(src, t0, gcol0 + l * KD, xn, hb, rstd, T)
                loaders = []
                for i in range(nblk):
                    def ld(buf, i=i):
                        v = wview(buf, KD, 2 * CB * 128)
                        wload(v[:, :, 0:CB * 128], wi2d, 0, D, i * CB * 128, CB * 128)
                        wload(v[:, :, CB * 128:2 * CB * 128], wi2d, 0, D, c.DFF + i * CB * 128, CB * 128)
                    loaders.append(ld)
                for mp in range(KD // 2):
                    for (kb, nk) in pcs:
                        def ld(buf, mp=mp, kb=kb, nk=nk):
                            wload(wview(buf, nk, 256), wo2d, kb * 128, nk * 128, mp * 256, 256)
                        loaders.append(ld)
                ws = WStream(P, wbufs, loaders)
                it = 0
                for i in range(nblk):
                    wv = wview(ws.get(i), KD, 2 * CB * 128)
                    for cc in range(CB):
                        ch = i * CB + cc
                        for tg in range(TG):
                            pg = P.bank(2 * (it % 2))
                            pu = P.bank(2 * (it % 2) + 1)
                            s = stmp[it % 2]
                            it += 1
                            xs_ = slice(tg * 512, (tg + 1) * 512)
                            for k in range(KD):
                                P.mm(pg, wv[:, k, cc * 128:(cc + 1) * 128], xn[:, k, xs_], start=(k == 0), stop=(k == KD - 1))
                            for k in range(KD):
                                P.mm(pu, wv[:, k, (CB + cc) * 128:(CB + cc + 1) * 128], xn[:, k, xs_], start=(k == 0), stop=(k == KD - 1))
                            P.act(s, pg, AF.Silu)
                            P.tt('dve', actT[:, ch, xs_], s, pu, ALU.mult)
                it = 0
                widx = nblk
                for mp in range(KD // 2):
                    tiles = []
                    for pi in range(npc):
                        tiles.append(wview(ws.get(widx, live=npc), pcs[pi][1], 256))
                        widx += 1
                    for mi in range(2):
                        m = mp * 2 + mi
                        for tg in range(TG):
                            po = P.bank(4 + (it % 2))
                            hr = hres[it % 2]
                            hn = hnew[it % 2]
                            it += 1
                            xs_ = slice(tg * 512, (tg + 1) * 512)
                            P.dma('sp', hr, src[m * 128:(m + 1) * 128, t0 + tg * 512:t0 + (tg + 1) * 512])
                            for pi, (kb, nk) in enumerate(pcs):
                                for kk in range(nk):
                                    P.mm(po, tiles[pi][:, kk, mi * 128:(mi + 1) * 128], actT[:, kb + kk, xs_],
                                         start=(kb + kk == 0), stop=(kb + kk == KF - 1))
                            P.stt('dve', hn, po, 0.5, hr, ALU.mult, ALU.add)
                            P.dma('sp', hT[m * 128:(m + 1) * 128, t0 + tg * 512:t0 + (tg + 1) * 512], hn)
            P.release(m0)

        def inproj_phase(l):
            m0 = P.mark()
            xn = P.alloc([KD, T], BF16)
            hb = [P.alloc([T], F32) for _ in range(2)]
            rstd = P.alloc([T], F32)
            NTB = T // 128
            avst = P.alloc([NTB, c.AW], F32)
            og = [P.alloc([T], F32) for _ in range(2)]
            ogb = [P.alloc([T], BF16) for _ in range(2)]
            otm = [P.alloc([512], F32) for _ in range(2)]
            otmb = [P.alloc([512], BF16) for _ in range(2)]
            t1 = [P.alloc([512], F32) for _ in range(2)]
            lng = P.alloc([c.AW], F32)
            lnb = P.alloc([c.AW], F32)
            st4 = P.alloc([8], F32)
            junk = P.alloc([c.AW], F32)
            P.dma('sp', lng, rowvec[:, l * c.NRV + c.rv_lng:l * c.NRV + c.rv_lng + c.AW])
            P.dma('sp', lnb, rowvec[:, l * c.NRV + c.rv_lnb:l * c.NRV + c.rv_lnb + c.AW])
            segs = [('au', c.o_au, c.AW, 'fm'), ('av', c.o_av, c.AW, 'tm'), ('q', c.o_q, c.BW, 'fm'),
                    ('k', c.o_k, c.BW, 'fm'), ('v', c.o_v, c.BW, 'tm'), ('z', c.o_z, c.CI, 'tm'),
                    ('xbc', c.o_xbc, c.CC, 'fm'), ('dt', c.o_dt, c.CH, 'tm'), ('g', c.o_g, 3 * D, 'fm')]
            groups = []
            for (nm, o, w, lay) in segs:
                for c0 in range(0, w, 512):
                    groups.append((nm, o, c0, min(512, w - c0), lay))
            for t0 in range(0, NT, T):
                norm_to_xn(hT, t0, c.cv_mix + l * KD, xn, hb, rstd, T)
                loaders = []
                for (nm, o, c0, ncol, lay) in groups:
                    def ld(buf, o=o, c0=c0, ncol=ncol):
                        wload(wview(buf, KD, ncol), win_b, 0, D, o + c0, ncol)
                    loaders.append(ld)
                ws = WStream(P, wbufs, loaders)
                it = 0
                for gi, (nm, o, c0, ncol, lay) in enumerate(groups):
                    wv = wview(ws.get(gi), KD, ncol)
                    if lay == 'fm':
                        for cc in range(ncol // 128):
                            row = c0 + cc * 128
                            stg = og[it % 2]
                            stgb = ogb[it % 2]
                            for tg in range(TG):
                                pb = P.bank(it % 4)
                                tt1 = t1[it % 2]
                                it += 1
                                xs_ = slice(tg * 512, (tg + 1) * 512)
                                for k in range(KD):
                                    P.mm(pb, wv[:, k, cc * 128:(cc + 1) * 128], xn[:, k, xs_], start=(k == 0), stop=(k == KD - 1))
                                if nm == 'au':
                                    P.act(tt1, pb, AF.Square)
                                    P.ts('dve', tt1, tt1, 0.044715, 1.0, ALU.mult, ALU.add)
                                    P.tt('dve', tt1, tt1, pb, ALU.mult)
                                    P.act(tt1, tt1, AF.Sigmoid, scale=1.5957691216)
                                    P.tt('dve', stg[:, xs_], tt1, pb, ALU.mult)
                                elif nm == 'q':
                                    P.act(stgb[:, xs_], pb, AF.Copy, scale=0.125)
                                elif nm == 'k':
                                    P.copy('dve', stgb[:, xs_], pb)
                                elif nm == 'xbc':
                                    P.copy('act', stg[:, xs_], pb)
                                elif nm == 'g':
                                    P.act(stg[:, xs_], pb, AF.Sigmoid)
                            rs = slice(row, row + 128)
                            if nm == 'au':
                                P.dma('sp', uT[rs, t0:t0 + T], stg)
                            elif nm == 'q':
                                P.dma('sp', qT[rs, t0:t0 + T], stgb)
                            elif nm == 'k':
                                P.dma('sp', kT[rs, t0:t0 + T], stgb)
                            elif nm == 'xbc':
                                P.dma('sp', xbcT[rs, 4 + t0:4 + t0 + T], stg)
                            elif nm == 'g':
                                P.dma('sp', gT[rs, t0:t0 + T], stg)
                    else:
                        for tb in range(NTB):
                            pb = P.bank(it % 4)
                            so = otm[it % 2]
                            sob = otmb[it % 2]
                            tt1 = t1[it % 2]
                            it += 1
                            for k in range(KD):
                                P.mm(pb[:, 0:ncol], xn[:, k, tb * 128:(tb + 1) * 128], wv[:, k, :], start=(k == 0), stop=(k == KD - 1))
                            pv = pb[:, 0:ncol]
                            rs = slice(t0 + tb * 128, t0 + (tb + 1) * 128)
                            if nm == 'av':
                                tv = tt1[:, 0:ncol]
                                P.act(tv, pv, AF.Square)
                                P.ts('dve', tv, tv, 0.044715, 1.0, ALU.mult, ALU.add)
                                P.tt('dve', tv, tv, pv, ALU.mult)
                                P.act(tv, tv, AF.Sigmoid, scale=1.5957691216)
                                P.tt('dve', avst[:, tb, c0:c0 + ncol], tv, pv, ALU.mult)
                            elif nm == 'v':
                                P.copy('dve', sob[:, 0:ncol], pv)
                                P.dma('sp', vA[rs, c0:c0 + ncol], sob[:, 0:ncol])
                            elif nm == 'z':
                                P.act(so[:, 0:ncol], pv, AF.Silu)
                                P.dma('sp', szD[rs, c0:c0 + ncol], so[:, 0:ncol])
                            elif nm == 'dt':
                                P.copy('dve', so[:, 0:ncol], pv)
                                P.dma('sp', dtr[rs, 0:ncol], so[:, 0:ncol])
                        if nm == 'av' and c0 + ncol == c.AW:
                            for tb in range(NTB):
                                a = avst[:, tb, :]
                                mean, ssq, msq, var = st4[:, 0:1], st4[:, 1:2], st4[:, 2:3], st4[:, 3:4]
                                P.rsum(mean, a)
                                P.act(junk, a, AF.Square)
                                P.rsum(ssq, junk)
                                P.ts('dve', mean, mean, 1.0 / c.AW, 0.0, ALU.mult, ALU.add)
                                P.tt('dve', msq, mean, mean, ALU.mult)
                                P.stt('dve', var, ssq, 1.0 / c.AW, msq, ALU.mult, ALU.subtract)
                                P.act(var, var, AF.Sqrt, bias=EPS)
                                P.recip(var, var)
                                P.ts('dve', a, a, mean, var, ALU.subtract, ALU.mult)
                                P.tt('dve', a, a, lng, ALU.mult)
                                vb = junk.bitcast(BF16)[:, 0:c.AW]
                                P.tt('dve', vb, a, lnb, ALU.add)
                                P.dma('sp', vS[t0 + tb * 128:t0 + (tb + 1) * 128, :], vb)
            P.release(m0)

        def sgu_phase(l):
            m0 = P.mark()
            AG = c.AG
            wf = P.alloc([AG, 128], F32)
            wb_ = P.alloc([AG, 128], BF16)
            brow = P.alloc([AG * 128], F32)
            P.dma('sp', wf, sguwT[l * AG * 128:(l + 1) * AG * 128, :].rearrange('(g s) t -> s g t', s=128))
            P.dma('sp', brow[0:1, :], sgub[l:l + 1, :])
            P.tt('dve', wb_, wf, bc(tri.unsqueeze(1), [128, AG, 128]), ALU.mult)
            ut = [P.alloc([AG, 512], F32) for _ in range(2)]
            vt = [P.alloc([4, c.AW], BF16) for _ in range(2)]
            ya = [P.alloc([AG, 512], BF16) for _ in range(2)]
            it = 0
            for n4 in range(NT // 512):
                cs = slice(n4 * 512, (n4 + 1) * 512)
                u_, v_, y_ = ut[n4 % 2], vt[n4 % 2], ya[n4 % 2]
                P.dma('sp', u_, uT[:, cs].rearrange('(g p) t -> p g t', p=128))
                P.dma('sp', v_, vS[n4 * 512:(n4 + 1) * 512, :].rearrange('(b p) c -> p b c', p=128))
                for gg in range(AG):
                    pb = P.bank(it % 4)
                    it += 1
                    for b in range(4):
                        P.mm(pb[:, b * 128:(b + 1) * 128], v_[:, b, gg * 128:(gg + 1) * 128], wb_[:, gg, :], start=True, stop=False)
                        P.mm(pb[:, b * 128:(b + 1) * 128], ones_f[0:1, :], brow[0:1, gg * 128:(gg + 1) * 128], start=False, stop=True)
                    P.tt('dve', y_[:, gg, :], pb, u_[:, gg, :], ALU.mult)
                P.dma('sp', yaT[:, cs].rearrange('(g p) t -> p g t', p=128), y_)
            P.release(m0)

        def attn_phase(l):
            m0 = P.mark()
            lam_init = 0.8 - 0.6 * math.exp(-0.3 * l)
            NJ = NT // 128
            lamt = P.alloc([256], F32)
            lw = P.alloc([8], F32)
            P.dma('sp', lamt, rowvec[:, l * c.NRV + c.rv_lam:l * c.NRV + c.rv_lam + 256])
            P.tt('dve', lamt[:, 0:64], lamt[:, 0:64], lamt[:, 64:128], ALU.mult)
            P.tt('dve', lamt[:, 128:192], lamt[:, 128:192], lamt[:, 192:256], ALU.mult)
            P.rsum(lw[:, 0:1], lamt[:, 0:64])
            P.rsum(lw[:, 1:2], lamt[:, 128:192])
            P.act(lw[:, 0:2], lw[:, 0:2], AF.Exp)
            lc = c.cv_lamc + 3 * l
            P.stt('dve', lw[:, 2:3], lw[:, 1:2], cv[:, lc:lc + 1], lw[:, 0:1], ALU.add, ALU.subtract)
            neglam = lw[:, 2:3]
            cfin = 1.0 - lam_init
            kt = [P.alloc([NT], BF16) for _ in range(2)]
            qt_ = [P.alloc([NT], BF16) for _ in range(2)]
            vt = [P.alloc([NJ, 128], BF16) for _ in range(2)]
            bn = [P.alloc([5, 512], F32) for _ in range(2)]
            pT = [P.alloc([512], BF16) for _ in range(4)]
            tmp = [P.alloc([512], F32) for _ in range(2)]
            f1 = P.alloc([512], F32)
            f2 = P.alloc([512], F32)
            f3 = P.alloc([512], F32)
            yb = [P.alloc([512], BF16) for _ in range(2)]
            it = 0
            for h in range(c.BH):
                k_, q_, v_, b_ = kt[h % 2], qt_[h % 2], vt[h % 2], bn[h % 2]
                hs = slice(h * 128, (h + 1) * 128)
                P.dma('sp', k_, kT[hs, :])
                P.dma('sp', q_, qT[hs, :])
                P.dma('sp', v_, vA[:, hs].rearrange('(j p) e -> p j e', p=128))
                P.dma('sp', b_, biasnear[hs, :].rearrange('p (d q) -> p d q', q=512))
                c31 = cv[:, c.cv_c31 + h:c.cv_c31 + h + 1]
                for qi in range(NT // 512):
                    qb0 = 4 * qi
                    jl = qb0 + 3
                    qs = slice(qi * 512, (qi + 1) * 512)
                    po = [P.bank(2), P.bank(3)]
                    pss = [P.bank(4), P.bank(5)]
                    for j in range(jl + 1):
                        for i in range(2):
                            st = P.bank(it % 2)
                            p_ = pT[it % 4]
                            tm_ = tmp[it % 2]
                            it += 1
                            ds = slice(i * 64, (i + 1) * 64)
                            P.mm(st, k_[ds, j * 128:(j + 1) * 128], q_[ds, qs], start=True, stop=True)
                            if j <= qb0 - 2:
                                P.act(p_, st, AF.Exp, bias=c31)
                            else:
                                P.tt('dve', tm_, st, b_[:, j - qb0 + 1, :], ALU.add)
                                P.act(p_, tm_, AF.Exp)
                            P.mm(po[i], v_[:, j, :], p_, start=(j == 0), stop=(j == jl))
                            P.mm(pss[i], ones_b, p_, start=(j == 0), stop=(j == jl))
                    P.recip(f1, pss[0])
                    P.tt('dve', f1, f1, po[0], ALU.mult)
                    P.recip(f2, pss[1])
                    P.tt('dve', f2, f2, po[1], ALU.mult)
                    P.stt('dve', f1, f2, neglam, f1, ALU.mult, ALU.add)
                    P.act(f2, f1, AF.Square)
                    pq = P.bank(6)
                    P.mm(pq, ones_f, f2, start=True, stop=True)
                    P.act(f3, pq, AF.Sqrt, scale=cv[:, lc + 1:lc + 2], bias=cv[:, lc + 2:lc + 3])
                    P.recip(f3, f3)
                    y_ = yb[qi % 2]
                    P.stt('dve', y_, f1, cv[:, c.cv_sub + l:c.cv_sub + l + 1], f3, ALU.mult, ALU.mult)
                    P.dma('sp', ybT[hs, qs], y_)
            P.release(m0)

        def ssd_phase(l):
            m0 = P.mark()
            CH, CG, R, CI, NCC = c.CH, c.CG, c.R, c.CI, c.NCC
            RP = R * 64
            rv0 = l * c.NRV
            dtb = P.alloc([CH], F32)
            Arow = P.alloc([CH], F32)
            Drow = P.alloc([CH], F32)
            ssm = P.alloc([CI], F32)
            P.dma('sp', dtb, rowvec[:, rv0 + c.rv_dtb:rv0 + c.rv_dtb + CH])
            P.dma('sp', Arow, rowvec[:, rv0 + c.rv_alog:rv0 + c.rv_alog + CH])
            P.dma('sp', Drow, rowvec[:, rv0 + c.rv_dsk:rv0 + c.rv_dsk + CH])
            P.dma('sp', ssm, rowvec[:, rv0 + c.rv_ssm:rv0 + c.rv_ssm + CI])
            P.act(Arow, Arow, AF.Exp)
            P.ts('dve', Arow, Arow, -1.0, 0.0, ALU.mult, ALU.add)
            H = P.alloc([CI], F32)
            Hb = P.alloc([CI], BF16)
            P.memset('dve', H, 0.0)
            P.memset('dve', Hb, 0.0)
            xpre = [P.alloc([516], F32) for _ in range(2)]
            cacc = [P.alloc([512], F32) for _ in range(2)]
            xs_tok = P.alloc([4, CI], F32)
            B_tok = P.alloc([4, CG * 128], BF16)
            BT = P.alloc([CG, 512], BF16)
            CT = P.alloc([CG, 512], BF16)
            dtt = P.alloc([4, CH], F32)
            sm = P.alloc([8, CH], F32)
            Dt = P.alloc([R, 128], F32)
            arg = P.alloc([R, 128], F32)
            MT = P.alloc([CH, 128], BF16)
            xd = P.alloc([CI], BF16)
            xdd = P.alloc([CI], BF16)
            y = P.alloc([CI], F32)
            ytmp = P.alloc([CI], F32)
            szt = P.alloc([CI], F32)
            ycb = P.alloc([CI], BF16)
            ycs = P.alloc([CI // 128, 512], BF16)
            s1 = P.alloc([4], F32)
            cwc = c.cv_cw + l * 4 * NCC
            cbc = c.cv_cb + l * NCC
            it = 0
            for n4 in range(NT // 512):
                t0 = n4 * 512
                for ch in range(NCC):
                    xp = xpre[ch % 2]
                    ca = cacc[ch % 2]
                    P.dma('sp', xp[:, 0:515], xbcT[ch * 128:(ch + 1) * 128, 4 + t0 - 3:4 + t0 + 512])
                    P.ts('dve', ca, xp[:, 0:512], cv[:, cwc + ch:cwc + ch + 1], cv[:, cbc + ch:cbc + ch + 1], ALU.mult, ALU.add)
                    for j in range(1, 4):
                        P.stt('dve', ca, xp[:, j:j + 512], cv[:, cwc + j * NCC + ch:cwc + j * NCC + ch + 1], ca, ALU.mult, ALU.add)
                    P.act(ca, ca, AF.Silu)
                    nxs = CI // 128
                    if ch < nxs + CG:
                        pb = P.bank(it % 2)
                        it += 1
                        for b in range(4):
                            P.tr(pb[:, b * 128:(b + 1) * 128], ca[:, b * 128:(b + 1) * 128], ident_f)
                        src3 = pb.rearrange('p (b c) -> p b c', c=128)
                        if ch < nxs:
                            P.copy('act', xs_tok[:, :, ch * 128:(ch + 1) * 128], src3)
                        else:
                            gg = ch - nxs
                            P.copy('act', B_tok[:, :, gg * 128:(gg + 1) * 128], src3)
                            P.copy(PE_ALT, BT[:, gg, :], ca)
                    else:
                        gg = ch - nxs - CG
                        P.copy(PE_ALT, CT[:, gg, :], ca)
                P.dma('sp', dtt, dtr[t0:t0 + 512, :].rearrange('(b p) h -> p b h', p=128))
                for b in range(4):
                    bs = slice(b * 128, (b + 1) * 128)
                    xv, dtv, a_, ax, acs, eacs, cdec, dsc, w1 = (sm[:, i, :] for i in range(0, 8)) if False else (None,) * 9
                    xv = sm[:, 0, :]
                    dtv = sm[:, 1, :]
                    a_ = sm[:, 2, :]
                    acs = sm[:, 3, :]
                    eacs = sm[:, 4, :]
                    cdec = sm[:, 5, :]
                    dsc = sm[:, 6, :]
                    w1 = sm[:, 7, :]
                    P.tt('dve', xv, dtt[:, b, :], dtb, ALU.add)
                    P.ts('dve', dtv, xv, -1.0, 0.0, ALU.mult, ALU.add)
                    P.tt('dve', dtv, dtv, xv, ALU.min)
                    P.act(dtv, dtv, AF.Exp)
                    P.act(dtv, dtv, AF.Ln, bias=1.0)
                    P.stt('dve', dtv, xv, 0.0, dtv, ALU.max, ALU.add)
                    P.tt('dve', a_, dtv, Arow, ALU.mult)
                    pa_ = P.bank(2)
                    P.mm(pa_[:, 0:CH], tri, a_, start=True, stop=True)
                    P.copy('dve', acs, pa_[:, 0:CH])
                    P.act(eacs, acs, AF.Exp)
                    for gg in range(CG):
                        hs = slice(gg * R, (gg + 1) * R)
                        P.tt('dve', Dt, bc(a_[:, hs].unsqueeze(2), [128, R, 128]), bc(tri.unsqueeze(1), [128, R, 128]), ALU.mult)
                        nb_ = (R * 128 + 511) // 512
                        pbk = [P.bank(3 + i_) for i_ in range(nb_)]
                        Dtf = Dt.rearrange('p r l -> p (r l)')
                        for i_ in range(nb_):
                            w_ = min(512, R * 128 - i_ * 512)
                            P.mm(pbk[i_][:, 0:w_], ones_f, Dtf[:, i_ * 512:i_ * 512 + w_], start=True, stop=True)
                        hpb = 512 // 128
                        for i_ in range(nb_):
                            r0 = i_ * hpb
                            rn = min(hpb, R - r0)
                            pv = pbk[i_][:, 0:rn * 128].rearrange('p (r l) -> p r l', l=128)
                            hh = slice(gg * R + r0, gg * R + r0 + rn)
                            P.tt('dve', arg[:, r0:r0 + rn, :], pv, bc(acs[:, hh].unsqueeze(2), [128, rn, 128]), ALU.subtract)
                            P.act(cdec[:, hh], pv[:, :, 127], AF.Exp)
                            P.tt('dve', dsc[:, hh], pv[:, :, 127], acs[:, hh], ALU.subtract)
                        P.tt(PE_ALT, arg, arg, bc(maskneg.unsqueeze(1), [128, R, 128]), ALU.add)
                        P.act(arg, arg, AF.Exp)
                        pc_ = P.bank(5)
                        P.mm(pc_[:, 0:128], BT[:, gg, bs], CT[:, gg, bs], start=True, stop=True)
                        P.tt('dve', MT[:, hs, :], arg, bc(pc_[:, 0:128].unsqueeze(1), [128, R, 128]), ALU.mult)
                    P.act(dsc, dsc, AF.Exp)
                    P.tt('dve', w1, dtv, dsc, ALU.mult)
                    xs3 = xs_tok[:, b, :].rearrange('p (h e) -> p h e', e=64)
                    P.tt(PE_ALT, xd.rearrange('p (h e) -> p h e', e=64), xs3, bc(dtv.unsqueeze(2), [128, CH, 64]), ALU.mult)
                    P.tt('dve', xdd.rearrange('p (h e) -> p h e', e=64), xs3, bc(w1.unsqueeze(2), [128, CH, 64]), ALU.mult)
                    for gg in range(CG):
                        gs = slice(gg * RP, (gg + 1) * RP)
                        pyd = P.bank(6)
                        pyo = P.bank(7)
                        pst = P.bank(gg % 2)
                        for r in range(R):
                            hh = gg * R + r
                            P.mm(pyd[:, r * 64:(r + 1) * 64], MT[:, hh, :], xd[:, hh * 64:(hh + 1) * 64], start=True, stop=True)
                        P.mm(pyo[:, 0:RP], CT[:, gg, bs], Hb[:, gs], start=True, stop=True)
                        P.mm(pst[:, 0:RP], B_tok[:, b, gg * 128:(gg + 1) * 128], xdd[:, gs], start=True, stop=True)
                        yt3 = ytmp[:, gs].rearrange('p (r e) -> p r e', e=64)
                        P.tt('dve', yt3, pyo[:, 0:RP].rearrange('p (r e) -> p r e', e=64),
                             bc(eacs[:, gg * R:(gg + 1) * R].unsqueeze(2), [128, R, 64]), ALU.mult)
                        P.tt('dve', y[:, gs], pyd[:, 0:RP], ytmp[:, gs], ALU.add)
                        H3 = H[:, gs].rearrange('p (r e) -> p r e', e=64)
                        P.tt(PE_ALT, H3, H3, bc(cdec[:, gg * R:(gg + 1) * R].unsqueeze(2), [128, R, 64]), ALU.mult)
                        P.tt('dve', H[:, gs], H[:, gs], pst[:, 0:RP], ALU.add)
                        P.copy('act', Hb[:, gs], H[:, gs])
                    P.tt(PE_ALT, ytmp.rearrange('p (h e) -> p h e', e=64), xs3, bc(Drow.unsqueeze(2), [128, CH, 64]), ALU.mult)
                    P.tt(PE_ALT, y, y, ytmp, ALU.add)
                    P.dma('sp', szt, szD[t0 + b * 128:t0 + (b + 1) * 128, :])
                    P.tt('dve', y, y, szt, ALU.mult)
                    P.act(ytmp, y, AF.Square)
                    P.rsum(s1[:, 0:1], ytmp)
                    P.act(s1[:, 0:1], s1[:, 0:1], AF.Sqrt, scale=1.0 / CI, bias=EPS)
                    P.recip(s1[:, 0:1], s1[:, 0:1])
                    P.stt('dve', ycb, y, s1[:, 0:1], ssm, ALU.mult, ALU.mult)
                    nchk = CI // 128
                    for c8 in range(0, nchk, 8):
                        n8 = min(8, nchk - c8)
                        pt = P.bank(c8 // 8 % 2, BF16)
                        for i_ in range(n8):
                            P.tr(pt[:, i_ * 128:(i_ + 1) * 128], ycb[:, (c8 + i_) * 128:(c8 + i_ + 1) * 128], ident_b)
                        P.copy('act', ycs[:, c8:c8 + n8, bs], pt[:, 0:n8 * 128].rearrange('p (c t) -> p c t', t=128))
                P.dma('sp', ycT[:, t0:t0 + 512].rearrange('(c p) t -> p c t', p=128), ycs)
            P.release(m0)

        def merge_phase(l):
            m0 = P.mark()
            AG, BH, NCI = c.AG, c.BH, c.CI // 128
            yat = P.alloc([AG, T], BF16)
            ybt = P.alloc([BH, T], BF16)
            yct = P.alloc([NCI, T], BF16)
            mg = P.alloc([KD, T], BF16)
            gt = [[P.alloc([512], F32) for _ in range(3)] for _ in range(2)]
            ta = [P.alloc([512], F32) for _ in range(2)]
            tb_ = [P.alloc([512], F32) for _ in range(2)]
            hres = [P.alloc([512], F32) for _ in range(2)]
            hnew = [P.alloc([512], F32) for _ in range(2)]
            MG = min(4, KD)
            for t0 in range(0, NT, T):
                ts_ = slice(t0, t0 + T)
                P.dma('sp', yat, yaT[:, ts_].rearrange('(k p) t -> p k t', p=128))
                P.dma('sp', ybt, ybT[:, ts_].rearrange('(k p) t -> p k t', p=128))
                P.dma('sp', yct, ycT[:, ts_].rearrange('(k p) t -> p k t', p=128))
                loaders = []
                for m4 in range(KD // MG):
                    for (w2d, nk) in ((wpa_b, AG), (wpb_b, BH), (wpc_b, NCI)):
                        def ld(buf, w2d=w2d, nk=nk, m4=m4):
                            wload(wview(buf, nk, MG * 128), w2d, 0, nk * 128, m4 * MG * 128, MG * 128)
                        loaders.append(ld)
                for m4 in range(KD // MG):
                    def ld(buf, m4=m4):
                        wload(wview(buf, KD, MG * 128), wout_b, 0, D, m4 * MG * 128, MG * 128)
                    loaders.append(ld)
                ws = WStream(P, wbufs, loaders)
                it = 0
                widx = 0
                for m4 in range(KD // MG):
                    wa = wview(ws.get(widx, live=3), AG, MG * 128)
                    wb2 = wview(ws.get(widx + 1, live=3), BH, MG * 128)
                    wc = wview(ws.get(widx + 2, live=3), NCI, MG * 128)
                    widx += 3
                    for mi in range(MG):
                        m = m4 * MG + mi
                        ms = slice(mi * 128, (mi + 1) * 128)
                        for tg in range(TG):
                            xs_ = slice(tg * 512, (tg + 1) * 512)
                            g3 = gt[it % 2]
                            t_a, t_b = ta[it % 2], tb_[it % 2]
                            it += 1
                            for br in range(3):
                                P.dma('sp', g3[br], gT[br * D + m * 128:br * D + (m + 1) * 128, t0 + tg * 512:t0 + (tg + 1) * 512])
                            pa_, pb_, pc_ = P.bank(0), P.bank(1), P.bank(2)
                            for k in range(AG):
                                P.mm(pa_, wa[:, k, ms], yat[:, k, xs_], start=(k == 0), stop=(k == AG - 1))
                            for k in range(BH):
                                P.mm(pb_, wb2[:, k, ms], ybt[:, k, xs_], start=(k == 0), stop=(k == BH - 1))
                            for k in range(NCI):
                                P.mm(pc_, wc[:, k, ms], yct[:, k, xs_], start=(k == 0), stop=(k == NCI - 1))
                            P.tt('dve', t_a, pa_, g3[0], ALU.mult)
                            P.tt('dve', t_b, pb_, g3[1], ALU.mult)
                            P.tt(PE_ALT, t_a, t_a, t_b, ALU.add)
                            P.tt('dve', t_b, pc_, g3[2], ALU.mult)
                            P.tt(PE_ALT, mg[:, m, xs_], t_a, t_b, ALU.add)
                it = 0
                for m4 in range(KD // MG):
                    wo_ = wview(ws.get(widx), KD, MG * 128)
                    widx += 1
                    for mi in range(MG):
                        m = m4 * MG + mi
                        ms = slice(mi * 128, (mi + 1) * 128)
                        for tg in range(TG):
                            xs_ = slice(tg * 512, (tg + 1) * 512)
                            po = P.bank(4 + it % 2)
                            hr, hn = hres[it % 2], hnew[it % 2]
                            it += 1
                            P.dma('sp', hr, hT[m * 128:(m + 1) * 128, t0 + tg * 512:t0 + (tg + 1) * 512])
                            for k in range(KD):
                                P.mm(po, wo_[:, k, ms], mg[:, k, xs_], start=(k == 0), stop=(k == KD - 1))
                            P.tt('dve', hn, po, hr, ALU.add)
                            P.dma('sp', hT[m * 128:(m + 1) * 128, t0 + tg * 512:t0 + (tg + 1) * 512], hn)
            P.release(m0)

        def final_phase(src):
            m0 = P.mark()
            hb = [P.alloc([T], F32) for _ in range(2)]
            rstd = P.alloc([T], F32)
            for t0 in range(0, NT, T):
                norm_to_xn(src, t0, c.cv_fin, None, hb, rstd, T, out_f32=outT)
            P.release(m0)

        g.stages = []
        src = xT
        done = False
        for l in range(n_layers):
            cast_phase(l)
            ffn_phase(l, f1wi_b, f1wo_b, c.cv_f1, src)
            src = hT
            if stop_after == ('f1', l):
                done = True
                break
            inproj_phase(l)
            sgu_phase(l)
            attn_phase(l)
            ssd_phase(l)
            merge_phase(l)
            if stop_after == ('mix', l):
                done = True
                break
            ffn_phase(l, f2wi_b, f2wo_b, c.cv_f2, hT)
        final_phase(src)
        P.finish()
        print('ops:', {e: len(P.ops[e]) for e in ENGS}, flush=True)
        if P.dump is not None:
            with open(os.environ['KDUMP'], 'w') as fdump:
                for d in P.dump:
                    fdump.write(repr(d) + '\n')
        P.emit(block)
    return nc


def t5_bucket_np(n):
    n = np.maximum(n, 0)
    nf = np.maximum(n, 1).astype(np.float64)
    large = 16 + (np.log(nf / 16.0) / math.log(128 / 16) * 16).astype(np.int64)
    large = np.minimum(large, 31)
    return np.where(n < 16, n, large)


def prep_inputs(cfg, inp):
    c = cfg
    L, D, KD, NCC = c.L, c.D, c.KD, c.NCC
    f = lambda a: np.ascontiguousarray(np.asarray(a, dtype=np.float32))
    com = {}
    com['f1wi'] = f(inp['ffn1_wi']).reshape(L * D, 2 * c.DFF)
    com['f1wo'] = f(inp['ffn1_wo']).reshape(L * c.DFF, D)
    com['f2wi'] = f(inp['ffn2_wi']).reshape(L * D, 2 * c.DFF)
    com['f2wo'] = f(inp['ffn2_wo']).reshape(L * c.DFF, D)
    com['win'] = f(inp['w_in']).reshape(L * D, c.DIN)
    com['wpa'] = f(inp['w_pa']).reshape(L * c.AW, D)
    com['wpb'] = f(inp['w_pb']).reshape(L * c.BW, D)
    com['wpc'] = f(inp['w_pc']).reshape(L * c.CI, D)
    com['wout'] = f(inp['w_out']).reshape(L * D, D)
    cvv = np.zeros((128, c.NCV), np.float32)

    def fm(v, nk):
        v = f(v)
        return v.reshape(-1, nk, 128).transpose(2, 0, 1).reshape(128, -1)
    cvv[:, c.cv_f1:c.cv_f1 + L * KD] = fm(inp['ffn1_norm'], KD)
    cvv[:, c.cv_mix:c.cv_mix + L * KD] = fm(inp['mix_norm'], KD)
    cvv[:, c.cv_f2:c.cv_f2 + L * KD] = fm(inp['ffn2_norm'], KD)
    cvv[:, c.cv_fin:c.cv_fin + KD] = fm(inp['final_norm'], KD)
    cvv[:, c.cv_cw:c.cv_cw + L * 4 * NCC] = fm(f(inp['conv_w']).reshape(L * 4, c.CC), NCC)
    cvv[:, c.cv_cb:c.cv_cb + L * NCC] = fm(inp['conv_b'], NCC)
    cvv[:, c.cv_sub:c.cv_sub + L] = f(inp['diff_subln']).T
    rb = f(inp['rel_bias'])
    cvv[:, c.cv_c31:c.cv_c31 + c.BH] = np.broadcast_to(rb[31:32, :], (128, c.BH))
    lam0 = inp.get('_lam_layer0', 0)
    for l in range(L):
        lam_init = 0.8 - 0.6 * math.exp(-0.3 * (l + lam0))
        cfin = 1.0 - lam_init
        cvv[:, c.cv_lamc + 3 * l + 0] = -lam_init
        cvv[:, c.cv_lamc + 3 * l + 1] = 1.0 / (128.0 * cfin * cfin)
        cvv[:, c.cv_lamc + 3 * l + 2] = EPS / (cfin * cfin)
    com['colvec'] = cvv
    rv = np.zeros((L, c.NRV), np.float32)
    rv[:, c.rv_lng:c.rv_lng + c.AW] = f(inp['sgu_ln_g'])
    rv[:, c.rv_lnb:c.rv_lnb + c.AW] = f(inp['sgu_ln_b'])
    rv[:, c.rv_dtb:c.rv_dtb + c.CH] = f(inp['dt_bias'])
    rv[:, c.rv_alog:c.rv_alog + c.CH] = f(inp['a_log'])
    rv[:, c.rv_dsk:c.rv_dsk + c.CH] = f(inp['d_skip'])
    rv[:, c.rv_ssm:c.rv_ssm + c.CI] = f(inp['ssm_norm'])
    rv[:, c.rv_lam:c.rv_lam + 256] = f(inp['diff_lambda']).reshape(L, 256)
    com['rowvec'] = np.ascontiguousarray(np.broadcast_to(rv.reshape(1, L * c.NRV), (128, L * c.NRV)))
    com['sguwT'] = np.ascontiguousarray(f(inp['sgu_w']).transpose(0, 1, 3, 2)).reshape(L * c.AG * 128, 128)
    com['sgub'] = f(inp['sgu_b']).reshape(L, c.AG * 128)
    kk = np.arange(128)[:, None, None]
    dd = np.arange(5)[None, :, None] - 1
    qq = np.arange(512)[None, None, :]
    dist = (qq // 128 - dd) * 128 + (qq % 128) - kk
    bk = t5_bucket_np(dist)
    bn = np.empty((c.BH, 128, 5, 512), np.float32)
    for h in range(c.BH):
        bn[h] = rb[:, h][bk]
    bn[:, dist < 0] = NEG
    com['biasnear'] = bn.reshape(c.BH * 128, 5 * 512)
    cs = np.zeros((128, 3, 128), np.float32)
    s_ = np.arange(128)[:, None]
    t_ = np.arange(128)[None, :]
    cs[:, 0, :] = (s_ <= t_)
    cs[:, 1, :] = np.where(s_ <= t_, 0.0, NEG)
    cs[:, 2, :] = np.eye(128)
    com['cst'] = cs.reshape(128, 384)
    return com


_NC_CACHE = {}


def run(cfg, inp, n_cores, **bkw):
    key = (cfg.D, cfg.S, cfg.L, tuple(sorted(bkw.items())))
    if key not in _NC_CACHE:
        _NC_CACHE[key] = build(cfg, **bkw)
    nc = _NC_CACHE[key]
    com = prep_inputs(cfg, inp)
    x = np.asarray(inp['x'], dtype=np.float32)
    B = x.shape[0]
    in_maps = []
    for i in range(n_cores):
        b = i % B
        m = dict(com)
        m['xT'] = np.ascontiguousarray(x[b].T)
        in_maps.append(m)
    tr = bool(os.environ.get('KTRACE'))
    res = run_bass_kernel_spmd(nc, in_maps, core_ids=list(range(n_cores)), **({'trace': True} if tr else {}))
    if tr:
        print('exec_time_ns', res.exec_time_ns, flush=True)
    out = np.stack([np.ascontiguousarray(res.results[b]['outT'].T) for b in range(B)], axis=0)
    return out.astype(np.float32)


PER_LAYER = ['ffn1_norm', 'ffn1_wi', 'ffn1_wo', 'mix_norm', 'w_in', 'sgu_ln_g', 'sgu_ln_b', 'sgu_w', 'sgu_b',
             'diff_lambda', 'diff_subln', 'conv_w', 'conv_b', 'dt_bias', 'a_log', 'd_skip', 'ssm_norm',
             'w_pa', 'w_pb', 'w_pc', 'w_out', 'ffn2_norm', 'ffn2_wi', 'ffn2_wo']


def kernel_unfused(**inputs):
    L = 4
    cfg = Cfg(L=1)
    key = ('unfused',)
    if key not in _NC_CACHE:
        _NC_CACHE[key] = build(cfg, emit_h=True)
    nc = _NC_CACHE[key]
    x = np.asarray(inputs['x'], dtype=np.float32)
    B = x.shape[0]
    hTs = [np.ascontiguousarray(x[b].T) for b in range(B)]
    res = None
    for l in range(L):
        sub = {k: (np.asarray(v)[l:l + 1] if k in PER_LAYER else np.asarray(v)) for k, v in inputs.items()}
        sub['_lam_layer0'] = l
        com = prep_inputs(cfg, sub)
        in_maps = []
        for b in range(B):
            m = dict(com)
            m['xT'] = hTs[b]
            in_maps.append(m)
        res = run_bass_kernel_spmd(nc, in_maps, core_ids=list(range(B)))
        hTs = [np.ascontiguousarray(res.results[b]['hT']) for b in range(B)]
    out = np.stack([np.ascontiguousarray(res.results[b]['outT'].T) for b in range(B)], axis=0)
    return out.astype(np.float32)


def kernel(**inputs):
    return kernel_unfused(**inputs)
```

```python
import math
import os
import contextlib
import numpy as np
import concourse.bass as bass
import concourse.mybir as mybir
from concourse.bass_utils import run_bass_kernel_spmd

F32 = mybir.dt.float32
BF16 = mybir.dt.bfloat16
U8 = mybir.dt.uint8
AF = mybir.ActivationFunctionType
ALU = mybir.AluOpType
AX = mybir.AxisListType
EPS = 1e-6
NEG = -30000.0
PE_ALT = os.environ.get('KALT', 'dve')


def _esz(dt):
    return 4 if dt == F32 else (2 if dt == BF16 else 1)


class Cfg:
    def __init__(self, D=2048, S=4096, L=4, B=4, TG=2):
        self.D, self.S, self.L, self.B = D, S, L, B
        self.DFF = 11 * D // 4
        self.AW = D // 2
        self.AG = self.AW // 128
        self.BW = D // 2
        self.BH = self.BW // 128
        self.CI = D
        self.CH = self.CI // 64
        self.CG = 4
        self.R = self.CH // self.CG
        self.CC = self.CI + 2 * self.CG * 128
        self.DIN = 2 * self.AW + 3 * self.BW + self.CI + self.CC + self.CH + 3 * D
        self.KD = D // 128
        self.KF = self.DFF // 128
        self.NT = S
        self.TG = TG
        self.T = 512 * TG
        self.NCC = self.CC // 128
        self.o_au = 0
        self.o_av = self.AW
        self.o_q = 2 * self.AW
        self.o_k = self.o_q + self.BW
        self.o_v = self.o_k + self.BW
        self.o_z = self.o_v + self.BW
        self.o_xbc = self.o_z + self.CI
        self.o_dt = self.o_xbc + self.CC
        self.o_g = self.o_dt + self.CH
        L_, KD, NCC = L, self.KD, self.NCC
        c = 0
        self.cv_f1 = c; c += L_ * KD
        self.cv_mix = c; c += L_ * KD
        self.cv_f2 = c; c += L_ * KD
        self.cv_fin = c; c += KD
        self.cv_cw = c; c += L_ * 4 * NCC
        self.cv_cb = c; c += L_ * NCC
        self.cv_sub = c; c += L_
        self.cv_c31 = c; c += self.BH
        self.cv_lamc = c; c += 3 * L_
        self.NCV = c
        r = 0
        self.rv_lng = r; r += self.AW
        self.rv_lnb = r; r += self.AW
        self.rv_dtb = r; r += self.CH
        self.rv_alog = r; r += self.CH
        self.rv_dsk = r; r += self.CH
        self.rv_ssm = r; r += self.CI
        self.rv_lam = r; r += 256
        self.NRV = r


ENGS = ('pe', 'act', 'dve', 'pool', 'sp')


class Prog:
    def __init__(self, nc, big, ps, sem_handles, sbuf_bytes):
        self.nc = nc
        self.big = big
        self.ps = ps
        self.ops = {e: [] for e in ENGS}
        self.cnt = {e: 0 for e in ENGS}
        self.seen = {e: {} for e in ENGS}
        self.sem = {}
        it = iter(sem_handles)
        for e in ('pe', 'act', 'dve', 'pool'):
            self.sem[('eng', e)] = next(it)
        self.dq = {}
        for q, K in (('sp', int(os.environ.get('KSP', '4'))), ('pool', 12), ('act', 4)):
            self.dq[q] = {'n': 0, 'K': K}
            for i in range(K):
                self.sem[('dma', q, i)] = next(it)
        self.recs = {}
        self.dcols = {}
        self.ro = set()
        self.sp_ = 0
        self.sbuf_bytes = sbuf_bytes
        self.last_dma = {}
        self.pos = {}
        self.gi = 0
        self.dump = [] if os.environ.get('KDUMP') else None
        self.limit = int(os.environ.get('KLIMIT', '0')) or 10**12

    def alloc(self, free_shape, dt):
        n = 1
        for s in free_shape:
            n *= s
        nb = n * _esz(dt)
        nb = (nb + 63) // 64 * 64
        off = self.sp_
        self.sp_ += nb
        assert self.sp_ <= self.sbuf_bytes, ("SBUF overflow", self.sp_)
        v = self.big[:, off:off + n * _esz(dt)].bitcast(dt)
        if len(free_shape) == 2:
            v = v.rearrange('p (a b) -> p a b', b=free_shape[1])
        elif len(free_shape) == 3:
            v = v.rearrange('p (a b c) -> p a b c', b=free_shape[1], c=free_shape[2])
        return v

    def mark(self):
        return self.sp_

    def release(self, m):
        self.sp_ = m

    def bank(self, i, dt=F32):
        v = self.ps[:, i, :]
        if dt != F32:
            v = v.bitcast(dt)
        return v

    def dram(self, name, rows, cols, dt, kind='Internal', ro=False):
        t = self.nc.dram_tensor(name, [rows, cols], dt, kind=kind).ap()
        self.dcols[name] = cols
        if ro:
            self.ro.add(name)
        return t

    def _reg(self, ap):
        name = ap.tensor.name
        es = _esz(ap.dtype)
        pat = ap.ap
        off = ap.offset
        if name in self.dcols:
            ncols = self.dcols[name]
            r0 = off // ncols
            c0 = off % ncols
            rext = 0
            cext = 0
            for st, cn in pat:
                if cn <= 1 or st == 0:
                    continue
                if st % ncols == 0:
                    rext += (cn - 1) * (st // ncols)
                else:
                    cext += (cn - 1) * st
            assert c0 + cext < ncols, (name, off, pat)
            page = max(512, (ncols * es) // 16)
            return (name, r0, r0 + rext + 1, c0 * es, (c0 + cext + 1) * es, page)
        pstep = pat[0][0]
        r0 = off // pstep
        f0 = off % pstep
        ext = 0
        for st, cn in pat[1:]:
            if cn > 1 and st > 0:
                ext += (cn - 1) * st
        return (name, r0, r0 + pat[0][1], f0 * es, (f0 + ext + 1) * es, 2048)

    def _deps(self, eng, reads, writes, semkey, val):
        deps = {}
        rregs = [self._reg(a) for a in reads if a.tensor.name not in self.ro]
        wregs = [self._reg(a) for a in writes]
        for regs, isw in ((rregs, False), (wregs, True)):
            for (name, r0, r1, c0, c1, page) in regs:
                pages = self.recs.setdefault(name, {})
                for pg in range(c0 // page, (c1 - 1) // page + 1):
                    lst = pages.get(pg)
                    if not lst:
                        continue
                    for rec in lst:
                        if rec[0] >= r1 or rec[1] <= r0 or rec[2] >= c1 or rec[3] <= c0:
                            continue
                        if (not isw) and (not rec[4]):
                            continue
                        if rec[7] == 'pe' and eng == 'pe':
                            continue
                        if rec[7] == eng and eng != 'dmaq' and isw and not rec[4]:
                            continue
                        k = rec[5]
                        if deps.get(k, 0) < rec[6]:
                            deps[k] = rec[6]
        for regs, isw in ((rregs, False), (wregs, True)):
            for (name, r0, r1, c0, c1, page) in regs:
                pages = self.recs.setdefault(name, {})
                newrec = (r0, r1, c0, c1, isw, semkey, val, eng)
                for pg in range(c0 // page, (c1 - 1) // page + 1):
                    lst = pages.get(pg)
                    if lst is None:
                        pages[pg] = [newrec]
                        continue
                    if isw:
                        lst[:] = [r for r in lst if not (r[0] >= r0 and r[1] <= r1 and r[2] >= c0 and r[3] <= c1)]
                    else:
                        lst[:] = [r for r in lst if not ((not r[4]) and r[5] == semkey and r[0] >= r0 and r[1] <= r1 and r[2] >= c0 and r[3] <= c1)]
                    lst.append(newrec)
        return deps

    def _waits(self, eng, deps):
        out = []
        seen = self.seen[eng]
        for k, v in deps.items():
            if seen.get(k, 0) >= v:
                continue
            seen[k] = v
            out.append((k, v))
            if k[0] == 'eng':
                self.ops[k[1]][self.pos[k[1]][v]][3] = True
        return out

    def op(self, eng, fn, reads, writes):
        self.cnt[eng] += 1
        semkey = ('eng', eng)
        val = self.cnt[eng]
        deps = self._deps(eng, reads, writes, semkey, val)
        waits = self._waits(eng, deps)
        self.pos.setdefault(eng, {})[val] = len(self.ops[eng])
        self.gi += 1
        if self.dump is not None:
            import sys as _s
            self.dump.append((self.gi, eng, _s._getframe(2).f_lineno, list(waits), val))
        self.ops[eng].append([waits, fn, (semkey, 1), False, self.gi])

    def dma(self, q, out, in_):
        d = self.dq[q]
        i = d['n'] % d['K']
        rnd = d['n'] // d['K']
        d['n'] += 1
        semkey = ('dma', q, i)
        val = 16 * (rnd + 1)
        deps = self._deps('dmaq', [in_], [out], semkey, val)
        if rnd > 0:
            deps[semkey] = max(deps.get(semkey, 0), 16 * rnd)
        waits = self._waits(q, deps)
        self.gi += 1
        if self.dump is not None:
            import sys as _s
            self.dump.append((self.gi, 'dma-' + q, _s._getframe(1).f_lineno, list(waits), (semkey, val)))
        self.ops[q].append([waits, lambda e: e.dma_start(out=out, in_=in_), (semkey, 16, val), True, self.gi])
        self.last_dma[semkey] = val

    def finish(self):
        waits = self._waits('sp', dict(self.last_dma))
        self.ops['sp'].append([waits, None, None, False, 0])

    def emit(self, block):
        P = self

        vmap = {}
        for en in ('pe', 'act', 'dve', 'pool'):
            m = {}
            run_ = 0
            inv = {p: v for v, p in P.pos.get(en, {}).items()}
            for p, o in enumerate(P.ops[en]):
                if o[2] is not None and o[2][0][0] == 'eng' and o[3] and o[4] <= P.limit:
                    run_ += 1
                    m[inv[p]] = run_
            vmap[en] = m
        P.n_inc = {en: len(vmap[en]) for en in vmap}
        P.emitted = {}

        def run(e, lst):
            for waits, fn, inc, ms, gi in lst:
                if gi > P.limit:
                    continue
                if fn is None:
                    waits = list(P.emitted.items())
                elif inc[0][0] == 'dma':
                    P.emitted[inc[0]] = max(P.emitted.get(inc[0], 0), inc[2])
                for k, v in waits:
                    if k[0] == 'eng':
                        v = vmap[k[1]][v]
                    e.wait_ge(P.sem[k], v)
                if fn is None:
                    continue
                ins = fn(e)
                if ms:
                    ins.then_inc(P.sem[inc[0]], inc[1])

        @block.tensor
        def _(e):
            run(e, P.ops['pe'])

        @block.scalar
        def _(e):
            run(e, P.ops['act'])

        @block.vector
        def _(e):
            run(e, P.ops['dve'])

        @block.gpsimd
        def _(e):
            run(e, P.ops['pool'])

        @block.sync
        def _(e):
            run(e, P.ops['sp'])

    def mm(self, out, lhsT, rhs, start=True, stop=True):
        self.op('pe', lambda e: e.matmul(out, lhsT=lhsT, rhs=rhs, start=start, stop=stop), [lhsT, rhs], [out])

    def tr(self, out, in_, ident):
        self.op('pe', lambda e: e.transpose(out, in_, ident), [in_, ident], [out])

    def act(self, out, in_, func, bias=None, scale=None):
        kw = {}
        rd = [in_]
        if bias is not None:
            kw['bias'] = bias
            if not isinstance(bias, (int, float)):
                rd.append(bias)
        if scale is not None:
            kw['scale'] = scale
            if not isinstance(scale, (int, float)):
                rd.append(scale)
        self.op('act', lambda e: e.activation(out=out, in_=in_, func=func, **kw), rd, [out])

    def tt(self, eng, out, in0, in1, op):
        self.op(eng, lambda e: e.tensor_tensor(out=out, in0=in0, in1=in1, op=op), [in0, in1], [out])

    def ts(self, eng, out, in0, s1, s2, op0, op1):
        rd = [in0] + [s for s in (s1, s2) if not isinstance(s, (int, float))]
        self.op(eng, lambda e: e.tensor_scalar(out=out, in0=in0, scalar1=s1, scalar2=s2, op0=op0, op1=op1), rd, [out])

    def ts1(self, eng, out, in0, s1, op0):
        rd = [in0] + ([] if isinstance(s1, (int, float)) else [s1])
        self.op(eng, lambda e: e.tensor_single_scalar(out=out, in_=in0, scalar=s1, op=op0), rd, [out])

    def stt(self, eng, out, in0, scalar, in1, op0, op1):
        rd = [in0, in1] + ([] if isinstance(scalar, (int, float)) else [scalar])
        self.op(eng, lambda e: e.scalar_tensor_tensor(out=out, in0=in0, scalar=scalar, in1=in1, op0=op0, op1=op1), rd, [out])

    def copy(self, eng, out, in_):
        if eng == 'act':
            self.op('act', lambda e: e.copy(out=out, in_=in_), [in_], [out])
        else:
            self.op(eng, lambda e: e.tensor_copy(out=out, in_=in_), [in_], [out])

    def recip(self, out, in_):
        self.op('dve', lambda e: e.reciprocal(out=out, in_=in_), [in_], [out])

    def rsum(self, out, in_):
        self.op('dve', lambda e: e.reduce_sum(out=out, in_=in_, axis=AX.X), [in_], [out])

    def memset(self, eng, ap, val):
        self.op(eng, lambda e: e.memset(ap, val), [], [ap])


class WStream:
    NB = 3

    def __init__(self, P, bufs, loaders):
        self.P, self.bufs, self.loaders = P, bufs, loaders
        self.next = 0

    def get(self, i, live=1):
        lim = min(i + self.NB - live, len(self.loaders) - 1)
        while self.next <= lim:
            self.loaders[self.next](self.bufs[self.next % self.NB])
            self.next += 1
        return self.bufs[i % self.NB]


def bc(ap, shape):
    return ap.broadcast_to(list(shape))


def build(cfg, n_layers=None, stop_after=None, emit_h=False):
    c = cfg
    D, L, NT, T, TG, KD, KF = c.D, c.L, c.NT, c.T, c.TG, c.KD, c.KF
    if n_layers is None:
        n_layers = L
    nc = bass.Bass('TRN2', target_bir_lowering=False)
    SB = 204 * 1024

    class G:
        pass
    g = G()

    with contextlib.ExitStack() as es:
        big = es.enter_context(nc.sbuf_tensor('big', [128, SB], U8))
        ps = es.enter_context(nc.psum_tensor('ps', [128, 8, 512], F32))
        sems = [es.enter_context(nc.semaphore('s%d' % i)) for i in range(4 + 16 + 12 + 4)]
        block = es.enter_context(nc.Block())
        P = Prog(nc, big, ps, sems, SB)

        def din(name, rows, cols, dt=F32):
            return P.dram(name, rows, cols, dt, kind='ExternalInput', ro=True)
        xT = din('xT', D, NT)
        f1wi = din('f1wi', L * D, 2 * c.DFF)
        f1wo = din('f1wo', L * c.DFF, D)
        f2wi = din('f2wi', L * D, 2 * c.DFF)
        f2wo = din('f2wo', L * c.DFF, D)
        win = din('win', L * D, c.DIN)
        wpa = din('wpa', L * c.AW, D)
        wpb = din('wpb', L * c.BW, D)
        wpc = din('wpc', L * c.CI, D)
        wout = din('wout', L * D, D)
        colvec = din('colvec', 128, c.NCV)
        rowvec = din('rowvec', 128, L * c.NRV)
        sguwT = din('sguwT', L * c.AG * 128, 128)
        sgub = din('sgub', L, c.AG * 128)
        biasnear = din('biasnear', c.BH * 128, 5 * 512)
        cst = din('cst', 128, 3 * 128)
        outT = P.dram('outT', D, NT, F32, kind='ExternalOutput')
        hT = P.dram('hT', D, NT, F32, kind=('ExternalOutput' if emit_h else 'Internal'))
        uT = P.dram('uT', c.AW, NT, F32)
        vS = P.dram('vS', NT, c.AW, BF16)
        qT = P.dram('qT', c.BW, NT, BF16)
        kT = P.dram('kT', c.BW, NT, BF16)
        vA = P.dram('vA', NT, c.BW, BF16)
        szD = P.dram('szD', NT, c.CI, F32)
        xbcT = P.dram('xbcT', c.CC, NT + 4, F32)
        dtr = P.dram('dtr', NT, c.CH, F32)
        gT = P.dram('gT', 3 * D, NT, F32)
        yaT = P.dram('yaT', c.AW, NT, BF16)
        ybT = P.dram('ybT', c.BW, NT, BF16)
        ycT = P.dram('ycT', c.CI, NT, BF16)
        f1wi_b = P.dram('f1wi_b', D, 2 * c.DFF, BF16)
        f1wo_b = P.dram('f1wo_b', c.DFF, D, BF16)
        f2wi_b = P.dram('f2wi_b', D, 2 * c.DFF, BF16)
        f2wo_b = P.dram('f2wo_b', c.DFF, D, BF16)
        win_b = P.dram('win_b', D, c.DIN, BF16)
        wpa_b = P.dram('wpa_b', c.AW, D, BF16)
        wpb_b = P.dram('wpb_b', c.BW, D, BF16)
        wpc_b = P.dram('wpc_b', c.CI, D, BF16)
        wout_b = P.dram('wout_b', D, D, BF16)
        castmap = [(f1wi, f1wi_b, D, 2 * c.DFF), (f1wo, f1wo_b, c.DFF, D), (win, win_b, D, c.DIN),
                   (wpa, wpa_b, c.AW, D), (wpb, wpb_b, c.BW, D), (wpc, wpc_b, c.CI, D), (wout, wout_b, D, D),
                   (f2wi, f2wi_b, D, 2 * c.DFF), (f2wo, f2wo_b, c.DFF, D)]

        def cast_phase(l):
            m0 = P.mark()
            CW = 4096
            s32 = [P.alloc([CW], F32) for _ in range(2)]
            s16 = [P.alloc([CW], BF16) for _ in range(2)]
            it = 0
            for (src, dst, nr, ncl) in castmap:
                for r in range(0, nr, 128):
                    for c0 in range(0, ncl, CW):
                        w = min(CW, ncl - c0)
                        a32, a16 = s32[it % 2], s16[it % 2]
                        it += 1
                        P.dma('sp', a32[:, 0:w], src[l * nr + r:l * nr + r + 128, c0:c0 + w])
                        P.copy('act', a16[:, 0:w], a32[:, 0:w])
                        P.dma('sp', dst[r:r + 128, c0:c0 + w], a16[:, 0:w])
            P.release(m0)

        cv = P.alloc([c.NCV], F32)
        csts = P.alloc([3, 128], F32)
        ones_f = P.alloc([128], F32)
        ones_b = P.alloc([128], BF16)
        ident_b = P.alloc([128], BF16)
        wbufs = [P.alloc([8192], BF16) for _ in range(WStream.NB)]
        P.dma('sp', cv, colvec)
        P.dma('sp', csts, cst.rearrange('p (a b) -> p a b', b=128))
        tri = csts[:, 0, :]
        maskneg = csts[:, 1, :]
        ident_f = csts[:, 2, :]
        P.memset('dve', ones_f, 1.0)
        P.memset('dve', ones_b, 1.0)
        P.copy('dve', ident_b, ident_f)
        zt = P.alloc([4], F32)
        P.memset('dve', zt, 0.0)
        for ch in range(c.NCC):
            P.dma('sp', xbcT[ch * 128:(ch + 1) * 128, 0:4], zt)

        def wload(buf_view, src2d, r0, nrows, c0, ncols):
            P.dma('sp', buf_view, src2d[r0:r0 + nrows, c0:c0 + ncols].rearrange('(k p) n -> p k n', p=128))

        def wview(buf, nk, ncols):
            return buf[:, 0:nk * ncols].rearrange('p (k n) -> p k n', n=ncols)

        def norm_to_xn(src, t0, gcol, xn, hb, rstd, Tn, out_f32=None):
            tgn = Tn // 512
            for k in range(KD):
                hk = hb[k % 2]
                P.dma('sp', hk[:, 0:Tn], src[k * 128:(k + 1) * 128, t0:t0 + Tn])
                P.act(hk[:, 0:Tn], hk[:, 0:Tn], AF.Square)
                for tg in range(tgn):
                    P.mm(P.bank(6 + tg), ones_f, hk[:, tg * 512:(tg + 1) * 512], start=(k == 0), stop=(k == KD - 1))
            for tg in range(tgn):
                P.act(rstd[:, tg * 512:(tg + 1) * 512], P.bank(6 + tg), AF.Sqrt, scale=1.0 / D, bias=EPS)
            P.recip(rstd[:, 0:Tn], rstd[:, 0:Tn])
            for k in range(KD):
                hk = hb[k % 2]
                P.dma('sp', hk[:, 0:Tn], src[k * 128:(k + 1) * 128, t0:t0 + Tn])
                if out_f32 is None:
                    P.stt('dve', xn[:, k, 0:Tn], hk[:, 0:Tn], cv[:, gcol + k:gcol + k + 1], rstd[:, 0:Tn], ALU.mult, ALU.mult)
                else:
                    P.stt('dve', hk[:, 0:Tn], hk[:, 0:Tn], cv[:, gcol + k:gcol + k + 1], rstd[:, 0:Tn], ALU.mult, ALU.mult)
                    P.dma('sp', out_f32[k * 128:(k + 1) * 128, t0:t0 + Tn], hk[:, 0:Tn])

        def ffn_phase(l, wi2d, wo2d, gcol0, src):
            m0 = P.mark()
            xn = P.alloc([KD, T], BF16)
            actT = P.alloc([KF, T], BF16)
            hb = [P.alloc([T], F32) for _ in range(2)]
            rstd = P.alloc([T], F32)
            stmp = [P.alloc([512], F32) for _ in range(2)]
            hres = [P.alloc([512], F32) for _ in range(2)]
            hnew = [P.alloc([512], F32) for _ in range(2)]
            CB = 2 if KF % 2 == 0 else 1
            nblk = KF // CB
            npc = (KF + 21) // 22
            pcs = []
            kb = 0
            for pc in range(npc):
                nk = min(22, KF - kb)
                pcs.append((kb, nk))
                kb += nk
            for t0 in range(0, NT, T):
                norm_to_xn(src, t0, gcol0 + l * KD, xn, hb, rstd, T)
                loaders = []
                for i in range(nblk):
                    def ld(buf, i=i):
                        v = wview(buf, KD, 2 * CB * 128)
                        wload(v[:, :, 0:CB * 128], wi2d, 0, D, i * CB * 128, CB * 128)
                        wload(v[:, :, CB * 128:2 * CB * 128], wi2d, 0, D, c.DFF + i * CB * 128, CB * 128)
                    loaders.append(ld)
                for mp in range(KD // 2):
                    for (kb, nk) in pcs:
                        def ld(buf, mp=mp, kb=kb, nk=nk):
                            wload(wview(buf, nk, 256), wo2d, kb * 128, nk * 128, mp * 256, 256)
                        loaders.append(ld)
                ws = WStream(P, wbufs, loaders)
                it = 0
                for i in range(nblk):
                    wv = wview(ws.get(i), KD, 2 * CB * 128)
                    for cc in range(CB):
                        ch = i * CB + cc
                        for tg in range(TG):
                            pg = P.bank(2 * (it % 2))
                            pu = P.bank(2 * (it % 2) + 1)
                            s = stmp[it % 2]
                            it += 1
                            xs_ = slice(tg * 512, (tg + 1) * 512)
                            for k in range(KD):
                                P.mm(pg, wv[:, k, cc * 128:(cc + 1) * 128], xn[:, k, xs_], start=(k == 0), stop=(k == KD - 1))
                            for k in range(KD):
                                P.mm(pu, wv[:, k, (CB + cc) * 128:(CB + cc + 1) * 128], xn[:, k, xs_], start=(k == 0), stop=(k == KD - 1))
                            P.act(s, pg, AF.Silu)
                            P.tt('dve', actT[:, ch, xs_], s, pu, ALU.mult)
                it = 0
                widx = nblk
                for mp in range(KD // 2):
                    tiles = []
                    for pi in range(npc):
                        tiles.append(wview(ws.get(widx, live=npc), pcs[pi][1], 256))
                        widx += 1
                    for mi in range(2):
                        m = mp * 2 + mi
                        for tg in range(TG):
                            po = P.bank(4 + (it % 2))
                            hr = hres[it % 2]
                            hn = hnew[it % 2]
                            it += 1
                            xs_ = slice(tg * 512, (tg + 1) * 512)
                            P.dma('sp', hr, src[m * 128:(m + 1) * 128, t0 + tg * 512:t0 + (tg + 1) * 512])
                            for pi, (kb, nk) in enumerate(pcs):
                                for kk in range(nk):
                                    P.mm(po, tiles[pi][:, kk, mi * 128:(mi + 1) * 128], actT[:, kb + kk, xs_],
                                         start=(kb + kk == 0), stop=(kb + kk == KF - 1))
                            P.stt('dve', hn, po, 0.5, hr, ALU.mult, ALU.add)
                            P.dma('sp', hT[m * 128:(m + 1) * 128, t0 + tg * 512:t0 + (tg + 1) * 512], hn)
            P.release(m0)

        def inproj_phase(l):
            m0 = P.mark()
            xn = P.alloc([KD, T], BF16)
            hb = [P.alloc([T], F32) for _ in range(2)]
            rstd = P.alloc([T], F32)
            NTB = T // 128
            avst = P.alloc([NTB, c.AW], F32)
            og = [P.alloc([T], F32) for _ in range(2)]
            ogb = [P.alloc([T], BF16) for _ in range(2)]
            otm = [P.alloc([512], F32) for _ in range(2)]
            otmb = [P.alloc([512], BF16) for _ in range(2)]
            t1 = [P.alloc([512], F32) for _ in range(2)]
            lng = P.alloc([c.AW], F32)
            lnb = P.alloc([c.AW], F32)
            st4 = P.alloc([8], F32)
            junk = P.alloc([c.AW], F32)
            P.dma('sp', lng, rowvec[:, l * c.NRV + c.rv_lng:l * c.NRV + c.rv_lng + c.AW])
            P.dma('sp', lnb, rowvec[:, l * c.NRV + c.rv_lnb:l * c.NRV + c.rv_lnb + c.AW])
            segs = [('au', c.o_au, c.AW, 'fm'), ('av', c.o_av, c.AW, 'tm'), ('q', c.o_q, c.BW, 'fm'),
                    ('k', c.o_k, c.BW, 'fm'), ('v', c.o_v, c.BW, 'tm'), ('z', c.o_z, c.CI, 'tm'),
                    ('xbc', c.o_xbc, c.CC, 'fm'), ('dt', c.o_dt, c.CH, 'tm'), ('g', c.o_g, 3 * D, 'fm')]
            groups = []
            for (nm, o, w, lay) in segs:
                for c0 in range(0, w, 512):
                    groups.append((nm, o, c0, min(512, w - c0), lay))
            for t0 in range(0, NT, T):
                norm_to_xn(hT, t0, c.cv_mix + l * KD, xn, hb, rstd, T)
                loaders = []
                for (nm, o, c0, ncol, lay) in groups:
                    def ld(buf, o=o, c0=c0, ncol=ncol):
                        wload(wview(buf, KD, ncol), win_b, 0, D, o + c0, ncol)
                    loaders.append(ld)
                ws = WStream(P, wbufs, loaders)
                it = 0
                for gi, (nm, o, c0, ncol, lay) in enumerate(groups):
                    wv = wview(ws.get(gi), KD, ncol)
                    if lay == 'fm':
                        for cc in range(ncol // 128):
                            row = c0 + cc * 128
                            stg = og[it % 2]
                            stgb = ogb[it % 2]
                            for tg in range(TG):
                                pb = P.bank(it % 4)
                                tt1 = t1[it % 2]
                                it += 1
                                xs_ = slice(tg * 512, (tg + 1) * 512)
                                for k in range(KD):
                                    P.mm(pb, wv[:, k, cc * 128:(cc + 1) * 128], xn[:, k, xs_], start=(k == 0), stop=(k == KD - 1))
                                if nm == 'au':
                                    P.act(tt1, pb, AF.Square)
                                    P.ts('dve', tt1, tt1, 0.044715, 1.0, ALU.mult, ALU.add)
                                    P.tt('dve', tt1, tt1, pb, ALU.mult)
                                    P.act(tt1, tt1, AF.Sigmoid, scale=1.5957691216)
                                    P.tt('dve', stg[:, xs_], tt1, pb, ALU.mult)
                                elif nm == 'q':
                                    P.act(stgb[:, xs_], pb, AF.Copy, scale=0.125)
                                elif nm == 'k':
                                    P.copy('dve', stgb[:, xs_], pb)
                                elif nm == 'xbc':
                                    P.copy('act', stg[:, xs_], pb)
                                elif nm == 'g':
                                    P.act(stg[:, xs_], pb, AF.Sigmoid)
                            rs = slice(row, row + 128)
                            if nm == 'au':
                                P.dma('sp', uT[rs, t0:t0 + T], stg)
                            elif nm == 'q':
                                P.dma('sp', qT[rs, t0:t0 + T], stgb)
                            elif nm == 'k':
                                P.dma('sp', kT[rs, t0:t0 + T], stgb)
                            elif nm == 'xbc':
                                P.dma('sp', xbcT[rs, 4 + t0:4 + t0 + T], stg)
                            elif nm == 'g':
                                P.dma('sp', gT[rs, t0:t0 + T], stg)
                    else:
                        for tb in range(NTB):
                            pb = P.bank(it % 4)
                            so = otm[it % 2]
                            sob = otmb[it % 2]
                            tt1 = t1[it % 2]
                            it += 1
                            for k in range(KD):
                                P.mm(pb[:, 0:ncol], xn[:, k, tb * 128:(tb + 1) * 128], wv[:, k, :], start=(k == 0), stop=(k == KD - 1))
                            pv = pb[:, 0:ncol]
                            rs = slice(t0 + tb * 128, t0 + (tb + 1) * 128)
                            if nm == 'av':
                                tv = tt1[:, 0:ncol]
                                P.act(tv, pv, AF.Square)
                                P.ts('dve', tv, tv, 0.044715, 1.0, ALU.mult, ALU.add)
                                P.tt('dve', tv, tv, pv, ALU.mult)
                                P.act(tv, tv, AF.Sigmoid, scale=1.5957691216)
                                P.tt('dve', avst[:, tb, c0:c0 + ncol], tv, pv, ALU.mult)
                            elif nm == 'v':
                                P.copy('dve', sob[:, 0:ncol], pv)
                                P.dma('sp', vA[rs, c0:c0 + ncol], sob[:, 0:ncol])
                            elif nm == 'z':
                                P.act(so[:, 0:ncol], pv, AF.Silu)
                                P.dma('sp', szD[rs, c0:c0 + ncol], so[:, 0:ncol])
                            elif nm == 'dt':
                                P.copy('dve', so[:, 0:ncol], pv)
                                P.dma('sp', dtr[rs, 0:ncol], so[:, 0:ncol])
                        if nm == 'av' and c0 + ncol == c.AW:
                            for tb in range(NTB):
                                a = avst[:, tb, :]
                                mean, ssq, msq, var = st4[:, 0:1], st4[:, 1:2], st4[:, 2:3], st4[:, 3:4]
                                P.rsum(mean, a)
                                P.act(junk, a, AF.Square)
                                P.rsum(ssq, junk)
                                P.ts('dve', mean, mean, 1.0 / c.AW, 0.0, ALU.mult, ALU.add)
                                P.tt('dve', msq, mean, mean, ALU.mult)
                                P.stt('dve', var, ssq, 1.0 / c.AW, msq, ALU.mult, ALU.subtract)
                                P.act(var, var, AF.Sqrt, bias=EPS)
                                P.recip(var, var)
                                P.ts('dve', a, a, mean, var, ALU.subtract, ALU.mult)
                                P.tt('dve', a, a, lng, ALU.mult)
                                vb = junk.bitcast(BF16)[:, 0:c.AW]
                                P.tt('dve', vb, a, lnb, ALU.add)
                                P.dma('sp', vS[t0 + tb * 128:t0 + (tb + 1) * 128, :], vb)
            P.release(m0)

        def sgu_phase(l):
            m0 = P.mark()
            AG = c.AG
            wf = P.alloc([AG, 128], F32)
            wb_ = P.alloc([AG, 128], BF16)
            brow = P.alloc([AG * 128], F32)
            P.dma('sp', wf, sguwT[l * AG * 128:(l + 1) * AG * 128, :].rearrange('(g s) t -> s g t', s=128))
            P.dma('sp', brow[0:1, :], sgub[l:l + 1, :])
            P.tt('dve', wb_, wf, bc(tri.unsqueeze(1), [128, AG, 128]), ALU.mult)
            ut = [P.alloc([AG, 512], F32) for _ in range(2)]
            vt = [P.alloc([4, c.AW], BF16) for _ in range(2)]
            ya = [P.alloc([AG, 512], BF16) for _ in range(2)]
            it = 0
            for n4 in range(NT // 512):
                cs = slice(n4 * 512, (n4 + 1) * 512)
                u_, v_, y_ = ut[n4 % 2], vt[n4 % 2], ya[n4 % 2]
                P.dma('sp', u_, uT[:, cs].rearrange('(g p) t -> p g t', p=128))
                P.dma('sp', v_, vS[n4 * 512:(n4 + 1) * 512, :].rearrange('(b p) c -> p b c', p=128))
                for gg in range(AG):
                    pb = P.bank(it % 4)
                    it += 1
                    for b in range(4):
                        P.mm(pb[:, b * 128:(b + 1) * 128], v_[:, b, gg * 128:(gg + 1) * 128], wb_[:, gg, :], start=True, stop=False)
                        P.mm(pb[:, b * 128:(b + 1) * 128], ones_f[0:1, :], brow[0:1, gg * 128:(gg + 1) * 128], start=False, stop=True)
                    P.tt('dve', y_[:, gg, :], pb, u_[:, gg, :], ALU.mult)
                P.dma('sp', yaT[:, cs].rearrange('(g p) t -> p g t', p=128), y_)
            P.release(m0)

        def attn_phase(l):
            m0 = P.mark()
            lam_init = 0.8 - 0.6 * math.exp(-0.3 * l)
            NJ = NT // 128
            lamt = P.alloc([256], F32)
            lw = P.alloc([8], F32)
            P.dma('sp', lamt, rowvec[:, l * c.NRV + c.rv_lam:l * c.NRV + c.rv_lam + 256])
            P.tt('dve', lamt[:, 0:64], lamt[:, 0:64], lamt[:, 64:128], ALU.mult)
            P.tt('dve', lamt[:, 128:192], lamt[:, 128:192], lamt[:, 192:256], ALU.mult)
            P.rsum(lw[:, 0:1], lamt[:, 0:64])
            P.rsum(lw[:, 1:2], lamt[:, 128:192])
            P.act(lw[:, 0:2], lw[:, 0:2], AF.Exp)
            lc = c.cv_lamc + 3 * l
            P.stt('dve', lw[:, 2:3], lw[:, 1:2], cv[:, lc:lc + 1], lw[:, 0:1], ALU.add, ALU.subtract)
            neglam = lw[:, 2:3]
            cfin = 1.0 - lam_init
            kt = [P.alloc([NT], BF16) for _ in range(2)]
            qt_ = [P.alloc([NT], BF16) for _ in range(2)]
            vt = [P.alloc([NJ, 128], BF16) for _ in range(2)]
            bn = [P.alloc([5, 512], F32) for _ in range(2)]
            pT = [P.alloc([512], BF16) for _ in range(4)]
            tmp = [P.alloc([512], F32) for _ in range(2)]
            f1 = P.alloc([512], F32)
            f2 = P.alloc([512], F32)
            f3 = P.alloc([512], F32)
            yb = [P.alloc([512], BF16) for _ in range(2)]
            it = 0
            for h in range(c.BH):
                k_, q_, v_, b_ = kt[h % 2], qt_[h % 2], vt[h % 2], bn[h % 2]
                hs = slice(h * 128, (h + 1) * 128)
                P.dma('sp', k_, kT[hs, :])
                P.dma('sp', q_, qT[hs, :])
                P.dma('sp', v_, vA[:, hs].rearrange('(j p) e -> p j e', p=128))
                P.dma('sp', b_, biasnear[hs, :].rearrange('p (d q) -> p d q', q=512))
                c31 = cv[:, c.cv_c31 + h:c.cv_c31 + h + 1]
                for qi in range(NT // 512):
                    qb0 = 4 * qi
                    jl = qb0 + 3
                    qs = slice(qi * 512, (qi + 1) * 512)
                    po = [P.bank(2), P.bank(3)]
                    pss = [P.bank(4), P.bank(5)]
                    for j in range(jl + 1):
                        for i in range(2):
                            st = P.bank(it % 2)
                            p_ = pT[it % 4]
                            tm_ = tmp[it % 2]
                            it += 1
                            ds = slice(i * 64, (i + 1) * 64)
                            P.mm(st, k_[ds, j * 128:(j + 1) * 128], q_[ds, qs], start=True, stop=True)
                            if j <= qb0 - 2:
                                P.act(p_, st, AF.Exp, bias=c31)
                            else:
                                P.tt('dve', tm_, st, b_[:, j - qb0 + 1, :], ALU.add)
                                P.act(p_, tm_, AF.Exp)
                            P.mm(po[i], v_[:, j, :], p_, start=(j == 0), stop=(j == jl))
                            P.mm(pss[i], ones_b, p_, start=(j == 0), stop=(j == jl))
                    P.recip(f1, pss[0])
                    P.tt('dve', f1, f1, po[0], ALU.mult)
                    P.recip(f2, pss[1])
                    P.tt('dve', f2, f2, po[1], ALU.mult)
                    P.stt('dve', f1, f2, neglam, f1, ALU.mult, ALU.add)
                    P.act(f2, f1, AF.Square)
                    pq = P.bank(6)
                    P.mm(pq, ones_f, f2, start=True, stop=True)
                    P.act(f3, pq, AF.Sqrt, scale=cv[:, lc + 1:lc + 2], bias=cv[:, lc + 2:lc + 3])
                    P.recip(f3, f3)
                    y_ = yb[qi % 2]
                    P.stt('dve', y_, f1, cv[:, c.cv_sub + l:c.cv_sub + l + 1], f3, ALU.mult, ALU.mult)
                    P.dma('sp', ybT[hs, qs], y_)
            P.release(m0)

        def ssd_phase(l):
            m0 = P.mark()
            CH, CG, R, CI, NCC = c.CH, c.CG, c.R, c.CI, c.NCC
            RP = R * 64
            rv0 = l * c.NRV
            dtb = P.alloc([CH], F32)
            Arow = P.alloc([CH], F32)
            Drow = P.alloc([CH], F32)
            ssm = P.alloc([CI], F32)
            P.dma('sp', dtb, rowvec[:, rv0 + c.rv_dtb:rv0 + c.rv_dtb + CH])
            P.dma('sp', Arow, rowvec[:, rv0 + c.rv_alog:rv0 + c.rv_alog + CH])
            P.dma('sp', Drow, rowvec[:, rv0 + c.rv_dsk:rv0 + c.rv_dsk + CH])
            P.dma('sp', ssm, rowvec[:, rv0 + c.rv_ssm:rv0 + c.rv_ssm + CI])
            P.act(Arow, Arow, AF.Exp)
            P.ts('dve', Arow, Arow, -1.0, 0.0, ALU.mult, ALU.add)
            H = P.alloc([CI], F32)
            Hb = P.alloc([CI], BF16)
            P.memset('dve', H, 0.0)
            P.memset('dve', Hb, 0.0)
            xpre = [P.alloc([516], F32) for _ in range(2)]
            cacc = [P.alloc([512], F32) for _ in range(2)]
            xs_tok = P.alloc([4, CI], F32)
            B_tok = P.alloc([4, CG * 128], BF16)
            BT = P.alloc([CG, 512], BF16)
            CT = P.alloc([CG, 512], BF16)
            dtt = P.alloc([4, CH], F32)
            sm = P.alloc([8, CH], F32)
            Dt = P.alloc([R, 128], F32)
            arg = P.alloc([R, 128], F32)
            MT = P.alloc([CH, 128], BF16)
            xd = P.alloc([CI], BF16)
            xdd = P.alloc([CI], BF16)
            y = P.alloc([CI], F32)
            ytmp = P.alloc([CI], F32)
            szt = P.alloc([CI], F32)
            ycb = P.alloc([CI], BF16)
            ycs = P.alloc([CI // 128, 512], BF16)
            s1 = P.alloc([4], F32)
            cwc = c.cv_cw + l * 4 * NCC
            cbc = c.cv_cb + l * NCC
            it = 0
            for n4 in range(NT // 512):
                t0 = n4 * 512
                for ch in range(NCC):
                    xp = xpre[ch % 2]
                    ca = cacc[ch % 2]
                    P.dma('sp', xp[:, 0:515], xbcT[ch * 128:(ch + 1) * 128, 4 + t0 - 3:4 + t0 + 512])
                    P.ts('dve', ca, xp[:, 0:512], cv[:, cwc + ch:cwc + ch + 1], cv[:, cbc + ch:cbc + ch + 1], ALU.mult, ALU.add)
                    for j in range(1, 4):
                        P.stt('dve', ca, xp[:, j:j + 512], cv[:, cwc + j * NCC + ch:cwc + j * NCC + ch + 1], ca, ALU.mult, ALU.add)
                    P.act(ca, ca, AF.Silu)
                    nxs = CI // 128
                    if ch < nxs + CG:
                        pb = P.bank(it % 2)
                        it += 1
                        for b in range(4):
                            P.tr(pb[:, b * 128:(b + 1) * 128], ca[:, b * 128:(b + 1) * 128], ident_f)
                        src3 = pb.rearrange('p (b c) -> p b c', c=128)
                        if ch < nxs:
                            P.copy('act', xs_tok[:, :, ch * 128:(ch + 1) * 128], src3)
                        else:
                            gg = ch - nxs
                            P.copy('act', B_tok[:, :, gg * 128:(gg + 1) * 128], src3)
                            P.copy(PE_ALT, BT[:, gg, :], ca)
                    else:
                        gg = ch - nxs - CG
                        P.copy(PE_ALT, CT[:, gg, :], ca)
                P.dma('sp', dtt, dtr[t0:t0 + 512, :].rearrange('(b p) h -> p b h', p=128))
                for b in range(4):
                    bs = slice(b * 128, (b + 1) * 128)
                    xv, dtv, a_, ax, acs, eacs, cdec, dsc, w1 = (sm[:, i, :] for i in range(0, 8)) if False else (None,) * 9
                    xv = sm[:, 0, :]
                    dtv = sm[:, 1, :]
                    a_ = sm[:, 2, :]
                    acs = sm[:, 3, :]
                    eacs = sm[:, 4, :]
                    cdec = sm[:, 5, :]
                    dsc = sm[:, 6, :]
                    w1 = sm[:, 7, :]
                    P.tt('dve', xv, dtt[:, b, :], dtb, ALU.add)
                    P.ts('dve', dtv, xv, -1.0, 0.0, ALU.mult, ALU.add)
                    P.tt('dve', dtv, dtv, xv, ALU.min)
                    P.act(dtv, dtv, AF.Exp)
                    P.act(dtv, dtv, AF.Ln, bias=1.0)
                    P.stt('dve', dtv, xv, 0.0, dtv, ALU.max, ALU.add)
                    P.tt('dve', a_, dtv, Arow, ALU.mult)
                    pa_ = P.bank(2)
                    P.mm(pa_[:, 0:CH], tri, a_, start=True, stop=True)
                    P.copy('dve', acs, pa_[:, 0:CH])
                    P.act(eacs, acs, AF.Exp)
                    for gg in range(CG):
                        hs = slice(gg * R, (gg + 1) * R)
                        P.tt('dve', Dt, bc(a_[:, hs].unsqueeze(2), [128, R, 128]), bc(tri.unsqueeze(1), [128, R, 128]), ALU.mult)
                        nb_ = (R * 128 + 511) // 512
                        pbk = [P.bank(3 + i_) for i_ in range(nb_)]
                        Dtf = Dt.rearrange('p r l -> p (r l)')
                        for i_ in range(nb_):
                            w_ = min(512, R * 128 - i_ * 512)
                            P.mm(pbk[i_][:, 0:w_], ones_f, Dtf[:, i_ * 512:i_ * 512 + w_], start=True, stop=True)
                        hpb = 512 // 128
                        for i_ in range(nb_):
                            r0 = i_ * hpb
                            rn = min(hpb, R - r0)
                            pv = pbk[i_][:, 0:rn * 128].rearrange('p (r l) -> p r l', l=128)
                            hh = slice(gg * R + r0, gg * R + r0 + rn)
                            P.tt('dve', arg[:, r0:r0 + rn, :], pv, bc(acs[:, hh].unsqueeze(2), [128, rn, 128]), ALU.subtract)
                            P.act(cdec[:, hh], pv[:, :, 127], AF.Exp)
                            P.tt('dve', dsc[:, hh], pv[:, :, 127], acs[:, hh], ALU.subtract)
                        P.tt(PE_ALT, arg, arg, bc(maskneg.unsqueeze(1), [128, R, 128]), ALU.add)
                        P.act(arg, arg, AF.Exp)
                        pc_ = P.bank(5)
                        P.mm(pc_[:, 0:128], BT[:, gg, bs], CT[:, gg, bs], start=True, stop=True)
                        P.tt('dve', MT[:, hs, :], arg, bc(pc_[:, 0:128].unsqueeze(1), [128, R, 128]), ALU.mult)
                    P.act(dsc, dsc, AF.Exp)
                    P.tt('dve', w1, dtv, dsc, ALU.mult)
                    xs3 = xs_tok[:, b, :].rearrange('p (h e) -> p h e', e=64)
                    P.tt(PE_ALT, xd.rearrange('p (h e) -> p h e', e=64), xs3, bc(dtv.unsqueeze(2), [128, CH, 64]), ALU.mult)
                    P.tt('dve', xdd.rearrange('p (h e) -> p h e', e=64), xs3, bc(w1.unsqueeze(2), [128, CH, 64]), ALU.mult)
                    for gg in range(CG):
                        gs = slice(gg * RP, (gg + 1) * RP)
                        pyd = P.bank(6)
                        pyo = P.bank(7)
                        pst = P.bank(gg % 2)
                        for r in range(R):
                            hh = gg * R + r
                            P.mm(pyd[:, r * 64:(r + 1) * 64], MT[:, hh, :], xd[:, hh * 64:(hh + 1) * 64], start=True, stop=True)
                        P.mm(pyo[:, 0:RP], CT[:, gg, bs], Hb[:, gs], start=True, stop=True)
                        P.mm(pst[:, 0:RP], B_tok[:, b, gg * 128:(gg + 1) * 128], xdd[:, gs], start=True, stop=True)
                        yt3 = ytmp[:, gs].rearrange('p (r e) -> p r e', e=64)
                        P.tt('dve', yt3, pyo[:, 0:RP].rearrange('p (r e) -> p r e', e=64),
                             bc(eacs[:, gg * R:(gg + 1) * R].unsqueeze(2), [128, R, 64]), ALU.mult)
                        P.tt('dve', y[:, gs], pyd[:, 0:RP], ytmp[:, gs], ALU.add)
                        H3 = H[:, gs].rearrange('p (r e) -> p r e', e=64)
                        P.tt(PE_ALT, H3, H3, bc(cdec[:, gg * R:(gg + 1) * R].unsqueeze(2), [128, R, 64]), ALU.mult)
                        P.tt('dve', H[:, gs], H[:, gs], pst[:, 0:RP], ALU.add)
                        P.copy('act', Hb[:, gs], H[:, gs])
                    P.tt(PE_ALT, ytmp.rearrange('p (h e) -> p h e', e=64), xs3, bc(Drow.unsqueeze(2), [128, CH, 64]), ALU.mult)
                    P.tt(PE_ALT, y, y, ytmp, ALU.add)
                    P.dma('sp', szt, szD[t0 + b * 128:t0 + (b + 1) * 128, :])
                    P.tt('dve', y, y, szt, ALU.mult)
                    P.act(ytmp, y, AF.Square)
                    P.rsum(s1[:, 0:1], ytmp)
                    P.act(s1[:, 0:1], s1[:, 0:1], AF.Sqrt, scale=1.0 / CI, bias=EPS)
                    P.recip(s1[:, 0:1], s1[:, 0:1])
                    P.stt('dve', ycb, y, s1[:, 0:1], ssm, ALU.mult, ALU.mult)
                    nchk = CI // 128
                    for c8 in range(0, nchk, 8):
                        n8 = min(8, nchk - c8)
                        pt = P.bank(c8 // 8 % 2, BF16)
                        for i_ in range(n8):
                            P.tr(pt[:, i_ * 128:(i_ + 1) * 128], ycb[:, (c8 + i_) * 128:(c8 + i_ + 1) * 128], ident_b)
                        P.copy('act', ycs[:, c8:c8 + n8, bs], pt[:, 0:n8 * 128].rearrange('p (c t) -> p c t', t=128))
                P.dma('sp', ycT[:, t0:t0 + 512].rearrange('(c p) t -> p c t', p=128), ycs)
            P.release(m0)

        def merge_phase(l):
            m0 = P.mark()
            AG, BH, NCI = c.AG, c.BH, c.CI // 128
            yat = P.alloc([AG, T], BF16)
            ybt = P.alloc([BH, T], BF16)
            yct = P.alloc([NCI, T], BF16)
            mg = P.alloc([KD, T], BF16)
            gt = [[P.alloc([512], F32) for _ in range(3)] for _ in range(2)]
            ta = [P.alloc([512], F32) for _ in range(2)]
            tb_ = [P.alloc([512], F32) for _ in range(2)]
            hres = [P.alloc([512], F32) for _ in range(2)]
            hnew = [P.alloc([512], F32) for _ in range(2)]
            MG = min(4, KD)
            for t0 in range(0, NT, T):
                ts_ = slice(t0, t0 + T)
                P.dma('sp', yat, yaT[:, ts_].rearrange('(k p) t -> p k t', p=128))
                P.dma('sp', ybt, ybT[:, ts_].rearrange('(k p) t -> p k t', p=128))
                P.dma('sp', yct, ycT[:, ts_].rearrange('(k p) t -> p k t', p=128))
                loaders = []
                for m4 in range(KD // MG):
                    for (w2d, nk) in ((wpa_b, AG), (wpb_b, BH), (wpc_b, NCI)):
                        def ld(buf, w2d=w2d, nk=nk, m4=m4):
                            wload(wview(buf, nk, MG * 128), w2d, 0, nk * 128, m4 * MG * 128, MG * 128)
                        loaders.append(ld)
                for m4 in range(KD // MG):
                    def ld(buf, m4=m4):
                        wload(wview(buf, KD, MG * 128), wout_b, 0, D, m4 * MG * 128, MG * 128)
                    loaders.append(ld)
                ws = WStream(P, wbufs, loaders)
                it = 0
                widx = 0
                for m4 in range(KD // MG):
                    wa = wview(ws.get(widx, live=3), AG, MG * 128)
                    wb2 = wview(ws.get(widx + 1, live=3), BH, MG * 128)
                    wc = wview(ws.get(widx + 2, live=3), NCI, MG * 128)
                    widx += 3
                    for mi in range(MG):
                        m = m4 * MG + mi
                        ms = slice(mi * 128, (mi + 1) * 128)
                        for tg in range(TG):
                            xs_ = slice(tg * 512, (tg + 1) * 512)
                            g3 = gt[it % 2]
                            t_a, t_b = ta[it % 2], tb_[it % 2]
                            it += 1
                            for br in range(3):
                                P.dma('sp', g3[br], gT[br * D + m * 128:br * D + (m + 1) * 128, t0 + tg * 512:t0 + (tg + 1) * 512])
                            pa_, pb_, pc_ = P.bank(0), P.bank(1), P.bank(2)
                            for k in range(AG):
                                P.mm(pa_, wa[:, k, ms], yat[:, k, xs_], start=(k == 0), stop=(k == AG - 1))
                            for k in range(BH):
                                P.mm(pb_, wb2[:, k, ms], ybt[:, k, xs_], start=(k == 0), stop=(k == BH - 1))
                            for k in range(NCI):
                                P.mm(pc_, wc[:, k, ms], yct[:, k, xs_], start=(k == 0), stop=(k == NCI - 1))
                            P.tt('dve', t_a, pa_, g3[0], ALU.mult)
                            P.tt('dve', t_b, pb_, g3[1], ALU.mult)
                            P.tt(PE_ALT, t_a, t_a, t_b, ALU.add)
                            P.tt('dve', t_b, pc_, g3[2], ALU.mult)
                            P.tt(PE_ALT, mg[:, m, xs_], t_a, t_b, ALU.add)
                it = 0
                for m4 in range(KD // MG):
                    wo_ = wview(ws.get(widx), KD, MG * 128)
                    widx += 1
                    for mi in range(MG):
                        m = m4 * MG + mi
                        ms = slice(mi * 128, (mi + 1) * 128)
                        for tg in range(TG):
                            xs_ = slice(tg * 512, (tg + 1) * 512)
                            po = P.bank(4 + it % 2)
                            hr, hn = hres[it % 2], hnew[it % 2]
                            it += 1
                            P.dma('sp', hr, hT[m * 128:(m + 1) * 128, t0 + tg * 512:t0 + (tg + 1) * 512])
                            for k in range(KD):
                                P.mm(po, wo_[:, k, ms], mg[:, k, xs_], start=(k == 0), stop=(k == KD - 1))
                            P.tt('dve', hn, po, hr, ALU.add)
                            P.dma('sp', hT[m * 128:(m + 1) * 128, t0 + tg * 512:t0 + (tg + 1) * 512], hn)
            P.release(m0)

        def final_phase(src):
            m0 = P.mark()
            hb = [P.alloc([T], F32) for _ in range(2)]
            rstd = P.alloc([T], F32)
            for t0 in range(0, NT, T):
                norm_to_xn(src, t0, c.cv_fin, None, hb, rstd, T, out_f32=outT)
            P.release(m0)

        g.stages = []
        src = xT
        done = False
        for l in range(n_layers):
            cast_phase(l)
            ffn_phase(l, f1wi_b, f1wo_b, c.cv_f1, src)
            src = hT
            if stop_after == ('f1', l):
                done = True
                break
            inproj_phase(l)
            sgu_phase(l)
            attn_phase(l)
            ssd_phase(l)
            merge_phase(l)
            if stop_after == ('mix', l):
                done = True
                break
            ffn_phase(l, f2wi_b, f2wo_b, c.cv_f2, hT)
        final_phase(src)
        P.finish()
        print('ops:', {e: len(P.ops[e]) for e in ENGS}, flush=True)
        if P.dump is not None:
            with open(os.environ['KDUMP'], 'w') as fdump:
                for d in P.dump:
                    fdump.write(repr(d) + '\n')
        P.emit(block)
    return nc


def t5_bucket_np(n):
    n = np.maximum(n, 0)
    nf = np.maximum(n, 1).astype(np.float64)
    large = 16 + (np.log(nf / 16.0) / math.log(128 / 16) * 16).astype(np.int64)
    large = np.minimum(large, 31)
    return np.where(n < 16, n, large)


def prep_inputs(cfg, inp):
    c = cfg
    L, D, KD, NCC = c.L, c.D, c.KD, c.NCC
    f = lambda a: np.ascontiguousarray(np.asarray(a, dtype=np.float32))
    com = {}
    com['f1wi'] = f(inp['ffn1_wi']).reshape(L * D, 2 * c.DFF)
    com['f1wo'] = f(inp['ffn1_wo']).reshape(L * c.DFF, D)
    com['f2wi'] = f(inp['ffn2_wi']).reshape(L * D, 2 * c.DFF)
    com['f2wo'] = f(inp['ffn2_wo']).reshape(L * c.DFF, D)
    com['win'] = f(inp['w_in']).reshape(L * D, c.DIN)
    com['wpa'] = f(inp['w_pa']).reshape(L * c.AW, D)
    com['wpb'] = f(inp['w_pb']).reshape(L * c.BW, D)
    com['wpc'] = f(inp['w_pc']).reshape(L * c.CI, D)
    com['wout'] = f(inp['w_out']).reshape(L * D, D)
    cvv = np.zeros((128, c.NCV), np.float32)

    def fm(v, nk):
        v = f(v)
        return v.reshape(-1, nk, 128).transpose(2, 0, 1).reshape(128, -1)
    cvv[:, c.cv_f1:c.cv_f1 + L * KD] = fm(inp['ffn1_norm'], KD)
    cvv[:, c.cv_mix:c.cv_mix + L * KD] = fm(inp['mix_norm'], KD)
    cvv[:, c.cv_f2:c.cv_f2 + L * KD] = fm(inp['ffn2_norm'], KD)
    cvv[:, c.cv_fin:c.cv_fin + KD] = fm(inp['final_norm'], KD)
    cvv[:, c.cv_cw:c.cv_cw + L * 4 * NCC] = fm(f(inp['conv_w']).reshape(L * 4, c.CC), NCC)
    cvv[:, c.cv_cb:c.cv_cb + L * NCC] = fm(inp['conv_b'], NCC)
    cvv[:, c.cv_sub:c.cv_sub + L] = f(inp['diff_subln']).T
    rb = f(inp['rel_bias'])
    cvv[:, c.cv_c31:c.cv_c31 + c.BH] = np.broadcast_to(rb[31:32, :], (128, c.BH))
    lam0 = inp.get('_lam_layer0', 0)
    for l in range(L):
        lam_init = 0.8 - 0.6 * math.exp(-0.3 * (l + lam0))
        cfin = 1.0 - lam_init
        cvv[:, c.cv_lamc + 3 * l + 0] = -lam_init
        cvv[:, c.cv_lamc + 3 * l + 1] = 1.0 / (128.0 * cfin * cfin)
        cvv[:, c.cv_lamc + 3 * l + 2] = EPS / (cfin * cfin)
    com['colvec'] = cvv
    rv = np.zeros((L, c.NRV), np.float32)
    rv[:, c.rv_lng:c.rv_lng + c.AW] = f(inp['sgu_ln_g'])
    rv[:, c.rv_lnb:c.rv_lnb + c.AW] = f(inp['sgu_ln_b'])
    rv[:, c.rv_dtb:c.rv_dtb + c.CH] = f(inp['dt_bias'])
    rv[:, c.rv_alog:c.rv_alog + c.CH] = f(inp['a_log'])
    rv[:, c.rv_dsk:c.rv_dsk + c.CH] = f(inp['d_skip'])
    rv[:, c.rv_ssm:c.rv_ssm + c.CI] = f(inp['ssm_norm'])
    rv[:, c.rv_lam:c.rv_lam + 256] = f(inp['diff_lambda']).reshape(L, 256)
    com['rowvec'] = np.ascontiguousarray(np.broadcast_to(rv.reshape(1, L * c.NRV), (128, L * c.NRV)))
    com['sguwT'] = np.ascontiguousarray(f(inp['sgu_w']).transpose(0, 1, 3, 2)).reshape(L * c.AG * 128, 128)
    com['sgub'] = f(inp['sgu_b']).reshape(L, c.AG * 128)
    kk = np.arange(128)[:, None, None]
    dd = np.arange(5)[None, :, None] - 1
    qq = np.arange(512)[None, None, :]
    dist = (qq // 128 - dd) * 128 + (qq % 128) - kk
    bk = t5_bucket_np(dist)
    bn = np.empty((c.BH, 128, 5, 512), np.float32)
    for h in range(c.BH):
        bn[h] = rb[:, h][bk]
    bn[:, dist < 0] = NEG
    com['biasnear'] = bn.reshape(c.BH * 128, 5 * 512)
    cs = np.zeros((128, 3, 128), np.float32)
    s_ = np.arange(128)[:, None]
    t_ = np.arange(128)[None, :]
    cs[:, 0, :] = (s_ <= t_)
    cs[:, 1, :] = np.where(s_ <= t_, 0.0, NEG)
    cs[:, 2, :] = np.eye(128)
    com['cst'] = cs.reshape(128, 384)
    return com


_NC_CACHE = {}


def run(cfg, inp, n_cores, **bkw):
    key = (cfg.D, cfg.S, cfg.L, tuple(sorted(bkw.items())))
    if key not in _NC_CACHE:
        _NC_CACHE[key] = build(cfg, **bkw)
    nc = _NC_CACHE[key]
    com = prep_inputs(cfg, inp)
    x = np.asarray(inp['x'], dtype=np.float32)
    B = x.shape[0]
    in_maps = []
    for i in range(n_cores):
        b = i % B
        m = dict(com)
        m['xT'] = np.ascontiguousarray(x[b].T)
        in_maps.append(m)
    tr = bool(os.environ.get('KTRACE'))
    res = run_bass_kernel_spmd(nc, in_maps, core_ids=list(range(n_cores)), **({'trace': True} if tr else {}))
    if tr:
        print('exec_time_ns', res.exec_time_ns, flush=True)
    out = np.stack([np.ascontiguousarray(res.results[b]['outT'].T) for b in range(B)], axis=0)
    return out.astype(np.float32)


PER_LAYER = ['ffn1_norm', 'ffn1_wi', 'ffn1_wo', 'mix_norm', 'w_in', 'sgu_ln_g', 'sgu_ln_b', 'sgu_w', 'sgu_b',
             'diff_lambda', 'diff_subln', 'conv_w', 'conv_b', 'dt_bias', 'a_log', 'd_skip', 'ssm_norm',
             'w_pa', 'w_pb', 'w_pc', 'w_out', 'ffn2_norm', 'ffn2_wi', 'ffn2_wo']


def kernel_unfused(**inputs):
    L = 4
    cfg = Cfg(L=1)
    key = ('unfused',)
    if key not in _NC_CACHE:
        _NC_CACHE[key] = build(cfg, emit_h=True)
    nc = _NC_CACHE[key]
    x = np.asarray(inputs['x'], dtype=np.float32)
    B = x.shape[0]
    hTs = [np.ascontiguousarray(x[b].T) for b in range(B)]
    res = None
    for l in range(L):
        sub = {k: (np.asarray(v)[l:l + 1] if k in PER_LAYER else np.asarray(v)) for k, v in inputs.items()}
        sub['_lam_layer0'] = l
        com = prep_inputs(cfg, sub)
        in_maps = []
        for b in range(B):
            m = dict(com)
            m['xT'] = hTs[b]
            in_maps.append(m)
        res = run_bass_kernel_spmd(nc, in_maps, core_ids=list(range(B)))
        hTs = [np.ascontiguousarray(res.results[b]['hT']) for b in range(B)]
    out = np.stack([np.ascontiguousarray(res.results[b]['outT'].T) for b in range(B)], axis=0)
    return out.astype(np.float32)


def kernel(**inputs):
    if os.environ.get('KUNFUSED'):
        return kernel_unfused(**inputs)
    cfg = Cfg()
    return run(cfg, inputs, 4)
```
